# Optimizing a Trainium2 kernel written in Bass

```python
import jax, jax.numpy as jnp
from jax import lax
import numpy as np

D_MODEL = 2048
BATCH = 2
SEQ = 8192
DEPTH = 4

PLE_DIM = 256
D_FF = 5504
EPS = 1e-6
NEG_INF = -1e30
N_BRANCH = 4
BRANCH_WIDTH = 1024

RNN_WIDTH = 1024
RNN_HEADS = 16
RNN_HEAD_DIM = RNN_WIDTH // RNN_HEADS
CONV_WIDTH = 4
RG_C = 8.0

POOL_WIDTH = 1024
POOL_WINDOWS = (2, 4, 8, 16)
POOL_GROUP = POOL_WIDTH // len(POOL_WINDOWS)

HEAD_DIM = 128
KV_HEADS = 8
ATTN_PATTERNS = ((128, 1), (512, 4), (2048, 16))
N_GROUPS_C = len(ATTN_PATTERNS)
Q_HEADS = KV_HEADS * N_GROUPS_C
ATTN_BLOCK = 128
ATTN_WIDTH = KV_HEADS * HEAD_DIM

SG_WIDTH = 1024
SG_CHUNK = 128
SG_GROUPS = 8
SG_GROUP_DIM = SG_WIDTH // SG_GROUPS

IN_WIDTHS = (RNN_WIDTH, RNN_WIDTH, POOL_WIDTH, Q_HEADS * HEAD_DIM, ATTN_WIDTH, ATTN_WIDTH,
             2 * SG_WIDTH, N_BRANCH * D_MODEL)
IN_COLS = sum(IN_WIDTHS)

kernel_name = "hybrid_parallel_gated_trunk"


def rms_norm(x, g):
    xf = x.astype(jnp.float32)
    y = xf * lax.rsqrt(jnp.mean(xf * xf, axis=-1, keepdims=True) + EPS)
    return (y * g.astype(jnp.float32)).astype(x.dtype)


def swiglu(h, w1, w3, w2):
    return (jax.nn.silu(h @ w1) * (h @ w3)) @ w2


def causal_depthwise_conv(x, w, b):
    S = x.shape[1]
    xp = jnp.pad(x, ((0, 0), (CONV_WIDTH - 1, 0), (0, 0)))
    y = b
    for j in range(CONV_WIDTH):
        y = y + w[j] * xp[:, CONV_WIDTH - 1 - j: CONV_WIDTH - 1 - j + S]
    return y


def rg_lru(x, wa, ba, wx, bx, lam):
    B, S, _ = x.shape
    xh = x.reshape(B, S, RNN_HEADS, RNN_HEAD_DIM)
    r = jax.nn.sigmoid(jnp.einsum('bshi,hij->bshj', xh, wa).reshape(B, S, RNN_WIDTH) + ba)
    i_g = jax.nn.sigmoid(jnp.einsum('bshi,hij->bshj', xh, wx).reshape(B, S, RNN_WIDTH) + bx)
    log_a = (-RG_C * r.astype(jnp.float32)) * jax.nn.softplus(-lam.astype(jnp.float32))
    a = jnp.exp(log_a)
    mult = jnp.sqrt(-jnp.expm1(2.0 * log_a))
    u = mult * (i_g * x).astype(jnp.float32)

    def combine(c1, c2):
        a1, b1 = c1
        a2, b2 = c2
        return a1 * a2, a2 * b1 + b2

    _, h = lax.associative_scan(combine, (a, u), axis=1)
    return h.astype(x.dtype)


def multi_scale_pool(x, w_pool, scale):
    B, S, _ = x.shape
    xg = x.reshape(B, S, len(POOL_WINDOWS), POOL_GROUP).astype(jnp.float32)
    csum = jnp.cumsum(xg, axis=1)
    pos = jnp.arange(S)
    outs = []
    for g, win in enumerate(POOL_WINDOWS):
        c = csum[:, :, g]
        shifted = jnp.pad(c, ((0, 0), (win, 0), (0, 0)))[:, :S]
        cnt = jnp.minimum(pos + 1, win).astype(jnp.float32)[None, :, None]
        outs.append((c - shifted) / cnt - xg[:, :, g])
    pooled = jnp.stack(outs, axis=2)
    y = jnp.einsum('bsgc,gcd->bsgd', pooled, w_pool.astype(jnp.float32)).reshape(B, S, POOL_WIDTH)
    return (y * scale.astype(jnp.float32)).astype(x.dtype)


def dilated_window_attention(q, k, v, window, dilation):
    B, S, H, D = q.shape
    n_back = window // dilation
    assert n_back <= ATTN_BLOCK
    span = dilation * ATTN_BLOCK
    S_pad = -(-S // span) * span
    L = S_pad // dilation
    nb = L // ATTN_BLOCK
    pad = ((0, 0), (0, S_pad - S), (0, 0), (0, 0))

    def to_blocks(t):
        t = jnp.pad(t, pad).reshape(B, L, dilation, H, D).transpose(0, 2, 1, 3, 4)
        return t.reshape(B, dilation, nb, ATTN_BLOCK, H, D)

    def band(t):
        prev = jnp.pad(t, ((0, 0), (0, 0), (1, 0), (0, 0), (0, 0), (0, 0)))[:, :, :-1]
        return jnp.concatenate([prev, t], axis=3)

    qb = to_blocks(q)
    kband = band(to_blocks(k))
    vband = band(to_blocks(v))
    s = jnp.einsum('brnqhd,brnkhd->brnhqk', qb, kband) * (D ** -0.5)
    qi = jnp.arange(ATTN_BLOCK)[:, None] + ATTN_BLOCK
    ki = jnp.arange(2 * ATTN_BLOCK)[None, :]
    dist = qi - ki
    key_pos = jnp.arange(nb)[:, None] * ATTN_BLOCK + ki - ATTN_BLOCK
    mask = ((dist >= 0) & (dist <= n_back))[None] & (key_pos >= 0)[:, None, :]
    s = jnp.where(mask[None, None, :, None], s, NEG_INF)
    m = jnp.max(s, axis=-1, keepdims=True)
    pexp = jnp.exp(s - m)
    denom = jnp.sum(pexp, axis=-1, keepdims=True)
    o = jnp.einsum('brnhqk,brnkhd->brnqhd', pexp, vband) / denom.transpose(0, 1, 2, 4, 3, 5)
    lse = (m + jnp.log(denom))[..., 0].transpose(0, 1, 2, 4, 3)
    o = o.reshape(B, dilation, L, H, D).transpose(0, 2, 1, 3, 4).reshape(B, S_pad, H, D)[:, :S]
    lse = lse.reshape(B, dilation, L, H).transpose(0, 2, 1, 3).reshape(B, S_pad, H)[:, :S]
    return o, lse


def spatial_gating(z, norm_g, w_s, b_s):
    u, vv = jnp.split(z, 2, axis=-1)
    vv = rms_norm(vv, norm_g)
    B, S, _ = vv.shape
    nc = S // SG_CHUNK
    vc = vv.reshape(B, nc, SG_CHUNK, SG_GROUPS, SG_GROUP_DIM)
    tri = jnp.tril(jnp.ones((SG_CHUNK, SG_CHUNK), dtype=bool))
    ws = jnp.where(tri[None], w_s, 0.0)
    mixed = jnp.einsum('gts,bnsgc->bntgc', ws, vc) + b_s.T[None, None, :, :, None]
    return u * mixed.reshape(B, S, SG_WIDTH)


def setup_inputs(seed: int = 0) -> dict:
    key = jax.random.key(seed)
    ks = iter(jax.random.split(key, 40))

    def nrm(shape, scale):
        return jax.random.normal(next(ks), shape, jnp.float32) * scale

    def gain(shape, s=0.02):
        return 1.0 + nrm(shape, s)

    L, D, F = DEPTH, D_MODEL, D_FF
    x = nrm((BATCH, SEQ, D), 1.0)
    p = nrm((L, BATCH, SEQ, PLE_DIM), 1.0)
    ffn1_norm = gain((L, D))
    ffn1_w1 = nrm((L, D, F), D ** -0.5)
    ffn1_w3 = nrm((L, D, F), D ** -0.5)
    ffn1_w2 = nrm((L, F, D), F ** -0.5)
    mix_norm = gain((L, D))
    w_in = nrm((L, D, IN_COLS), D ** -0.5)
    b_gate = nrm((L, N_BRANCH, D), 0.02)
    conv_w = nrm((L, CONV_WIDTH, RNN_WIDTH), CONV_WIDTH ** -0.5)
    conv_b = nrm((L, RNN_WIDTH), 0.02)
    rg_wa = nrm((L, RNN_HEADS, RNN_HEAD_DIM, RNN_HEAD_DIM), RNN_HEAD_DIM ** -0.5)
    rg_ba = nrm((L, RNN_WIDTH), 0.02)
    rg_wx = nrm((L, RNN_HEADS, RNN_HEAD_DIM, RNN_HEAD_DIM), RNN_HEAD_DIM ** -0.5)
    rg_bx = nrm((L, RNN_WIDTH), 0.02)
    a0 = jax.random.uniform(next(ks), (L, RNN_WIDTH), jnp.float32, 0.9, 0.999)
    rg_lambda = jnp.log(a0) - jnp.log1p(-a0)
    pool_w = nrm((L, len(POOL_WINDOWS), POOL_GROUP, POOL_GROUP), POOL_GROUP ** -0.5)
    pool_scale = gain((L, POOL_WIDTH), 0.1)
    q_gain = gain((L, HEAD_DIM))
    k_gain = gain((L, HEAD_DIM))
    sg_norm = gain((L, SG_WIDTH))
    sg_w = nrm((L, SG_GROUPS, SG_CHUNK, SG_CHUNK), SG_CHUNK ** -0.5)
    sg_b = gain((L, SG_GROUPS, SG_CHUNK))
    w_branch = nrm((L, N_BRANCH, BRANCH_WIDTH, D), BRANCH_WIDTH ** -0.5)
    w_out = nrm((L, D, D), D ** -0.5)
    ffn2_norm = gain((L, D))
    ffn2_w1 = nrm((L, D, F), D ** -0.5)
    ffn2_w3 = nrm((L, D, F), D ** -0.5)
    ffn2_w2 = nrm((L, F, D), F ** -0.5)
    ple_norm = gain((L, D))
    ple_gate_w = nrm((L, D, D), D ** -0.5)
    ple_proj = nrm((L, PLE_DIM, D), PLE_DIM ** -0.5)
    return {"x": x, "p": p, "ffn1_norm": ffn1_norm, "ffn1_w1": ffn1_w1, "ffn1_w3": ffn1_w3,
            "ffn1_w2": ffn1_w2, "mix_norm": mix_norm, "w_in": w_in, "b_gate": b_gate,
            "conv_w": conv_w, "conv_b": conv_b, "rg_wa": rg_wa, "rg_ba": rg_ba, "rg_wx": rg_wx,
            "rg_bx": rg_bx, "rg_lambda": rg_lambda, "pool_w": pool_w, "pool_scale": pool_scale,
            "q_gain": q_gain, "k_gain": k_gain, "sg_norm": sg_norm, "sg_w": sg_w, "sg_b": sg_b,
            "w_branch": w_branch, "w_out": w_out, "ffn2_norm": ffn2_norm, "ffn2_w1": ffn2_w1,
            "ffn2_w3": ffn2_w3, "ffn2_w2": ffn2_w2, "ple_norm": ple_norm, "ple_gate_w": ple_gate_w,
            "ple_proj": ple_proj}


def reference(x, p, ffn1_norm, ffn1_w1, ffn1_w3, ffn1_w2, mix_norm, w_in, b_gate, conv_w, conv_b,
              rg_wa, rg_ba, rg_wx, rg_bx, rg_lambda, pool_w, pool_scale, q_gain, k_gain, sg_norm,
              sg_w, sg_b, w_branch, w_out, ffn2_norm, ffn2_w1, ffn2_w3, ffn2_w2, ple_norm,
              ple_gate_w, ple_proj):
    B, S, _ = x.shape
    split_idx = np.cumsum(IN_WIDTHS)[:-1].tolist()
    for i in range(DEPTH):
        x = x + 0.5 * swiglu(rms_norm(x, ffn1_norm[i]), ffn1_w1[i], ffn1_w3[i], ffn1_w2[i])

        h = rms_norm(x, mix_norm[i])
        proj = h @ w_in[i]
        xa, ga, xb, qf, kf, vf, zd, gates = jnp.split(proj, split_idx, axis=-1)

        ya = rg_lru(causal_depthwise_conv(xa, conv_w[i], conv_b[i]),
                    rg_wa[i], rg_ba[i], rg_wx[i], rg_bx[i], rg_lambda[i]) * jax.nn.gelu(ga)

        yb = multi_scale_pool(xb, pool_w[i], pool_scale[i])

        q = rms_norm(qf.reshape(B, S, Q_HEADS, HEAD_DIM), q_gain[i]).astype(jnp.float32)
        q = q.reshape(B, S, N_GROUPS_C, KV_HEADS, HEAD_DIM)
        k = rms_norm(kf.reshape(B, S, KV_HEADS, HEAD_DIM), k_gain[i]).astype(jnp.float32)
        v = vf.reshape(B, S, KV_HEADS, HEAD_DIM).astype(jnp.float32)
        outs, lses = [], []
        for g, (win, dil) in enumerate(ATTN_PATTERNS):
            o, l = dilated_window_attention(q[:, :, g], k, v, win, dil)
            outs.append(o)
            lses.append(l)
        wg = jax.nn.softmax(jnp.stack(lses, axis=0), axis=0)[..., None]
        yc = jnp.sum(wg * jnp.stack(outs, axis=0), axis=0).reshape(B, S, ATTN_WIDTH).astype(x.dtype)

        yd = spatial_gating(jax.nn.gelu(zd), sg_norm[i], sg_w[i], sg_b[i])

        gate = jax.nn.sigmoid(gates.reshape(B, S, N_BRANCH, D_MODEL) + b_gate[i])
        merged = gate[:, :, 0] * (ya @ w_branch[i, 0])
        merged = merged + gate[:, :, 1] * (yb @ w_branch[i, 1])
        merged = merged + gate[:, :, 2] * (yc @ w_branch[i, 2])
        merged = merged + gate[:, :, 3] * (yd @ w_branch[i, 3])
        x = x + merged @ w_out[i]

        x = x + 0.5 * swiglu(rms_norm(x, ffn2_norm[i]), ffn2_w1[i], ffn2_w3[i], ffn2_w2[i])

        pg = jax.nn.sigmoid(rms_norm(x, ple_norm[i]) @ ple_gate_w[i])
        x = x + pg * (p[i] @ ple_proj[i])
    return x
```

```python
import contextlib
import numpy as np
import concourse.bass as bass
import concourse.mybir as mybir
from concourse.bass_utils import run_bass_kernel_spmd

F32 = mybir.dt.float32
BF16 = mybir.dt.bfloat16
U8 = mybir.dt.uint8
AF = mybir.ActivationFunctionType
ALU = mybir.AluOpType

D = 2048
KC = 16
PLE = 256
EPS = 1e-6
IN_COLS = 18432
O_XA, O_GA, O_XB, O_Q, O_K, O_V, O_U, O_VV, O_G = 0, 1024, 2048, 3072, 6144, 7168, 8192, 9216, 10240
NVL = 208
V_F1, V_MIX, V_F2, V_PLE, V_BG, V_CW, V_CB, V_BA, V_BX, V_LAM, V_PS, V_QG, V_KG = (
    0, 16, 32, 48, 64, 128, 160, 168, 176, 184, 192, 200, 201)
SAME_ENGINE_SYNC = True
SPLIT = True


class Tok:
    __slots__ = ("sem", "val", "eng", "dma")

    def __init__(self, sem, val, eng, dma):
        self.sem, self.val, self.eng, self.dma = sem, val, eng, dma


class Tr:
    PERIOD = 24000
    DMA_POOL = 12
    DMA_MAX_USES = 1700

    def __init__(self, nc, stack):
        self.nc, self.stack = nc, stack
        self.engs = ["pe", "act", "dve", "pool", "sp"]
        self.recs = {e: [] for e in self.engs}
        self.count = {e: 0 for e in self.engs}
        self.esems = {e: [] for e in self.engs}
        self.seen = {e: {} for e in self.engs}
        self.last_w = {}
        self.readers = {}
        self.dpool = {e: [] for e in self.engs}
        self.dnext = {e: 0 for e in self.engs}
        self.pending = {}
        self.nsem = 0

    def _newsem(self, name):
        self.nsem += 1
        return self.stack.enter_context(self.nc.semaphore(f"{name}_{self.nsem}"))

    def _need(self, eng, tok, waits):
        if tok is None:
            return
        sid = id(tok.sem)
        if self.seen[eng].get(sid, 0) >= tok.val:
            return
        self.seen[eng][sid] = tok.val
        waits.append((tok.sem, tok.val))

    def op(self, eng, fn, reads=(), writes=(), dma=False):
        waits = []
        deps = []
        for k in reads:
            deps.append(self.last_w.get(k))
        for k in writes:
            deps.append(self.last_w.get(k))
            rd = self.readers.get(k)
            if rd:
                deps.extend(rd.values())
        for tkn in deps:
            if tkn is None:
                continue
            if (not dma) and (not tkn.dma) and tkn.eng == eng:
                if eng == "pe" or not SAME_ENGINE_SYNC:
                    continue
            self._need(eng, tkn, waits)
        if dma:
            pool = self.dpool[eng]
            if len(pool) < self.DMA_POOL:
                pool.append([self._newsem("d" + eng), 0])
            i = self.dnext[eng] % self.DMA_POOL
            self.dnext[eng] += 1
            if pool[i][1] >= self.DMA_MAX_USES:
                pool[i] = [self._newsem("d" + eng), 0]
            sem, uses = pool[i]
            if uses > 0:
                self._need(eng, Tok(sem, 16 * uses, eng, True), waits)
            pool[i][1] = uses + 1
            tok = Tok(sem, 16 * (uses + 1), eng, True)
            inc = 16
            self.pending[id(sem)] = tok
        else:
            n = self.count[eng]
            self.count[eng] = n + 1
            ep = n // self.PERIOD
            if ep >= len(self.esems[eng]):
                self.esems[eng].append(self._newsem("e" + eng))
            sem = self.esems[eng][ep]
            tok = Tok(sem, n % self.PERIOD + 1, eng, False)
            inc = 1
        for k in writes:
            self.last_w[k] = tok
            self.readers[k] = {}
        for k in reads:
            if k in writes:
                continue
            rd = self.readers.setdefault(k, {})
            rd[(eng, id(tok.sem)) if dma else eng] = tok
        self.recs[eng].append((waits, fn, (tok.sem, inc)))
        return tok

    def latest(self, eng):
        n = self.count[eng]
        if n == 0:
            return None
        n -= 1
        return Tok(self.esems[eng][n // self.PERIOD], n % self.PERIOD + 1, eng, False)

    def barrier(self, marker_fn):
        waits = []
        for e in self.engs:
            if e != "dve":
                self._need("dve", self.latest(e), waits)
        for tkn in self.pending.values():
            self._need("dve", tkn, waits)
        self.pending = {}
        self.recs["dve"].append((waits, None, None))
        self.last_w, self.readers = {}, {}
        b = self.op("dve", marker_fn)
        for e in self.engs:
            if e != "dve":
                w = []
                self._need(e, b, w)
                self.recs[e].append((w, None, None))

    def emit(self, eng, e):
        for waits, fn, sig in self.recs[eng]:
            for sem, val in waits:
                e.wait_ge(sem, val)
            if fn is not None:
                ins = fn(e)
                ins.then_inc(sig[0], sig[1])


class Rot:
    def __init__(self, name, aps, keys=None):
        self.name, self.aps, self.i = name, aps, 0
        self.keys = keys if keys is not None else [(name, i) for i in range(len(aps))]

    def next(self):
        i = self.i % len(self.aps)
        self.i += 1
        return self.aps[i], self.keys[i]


def build_program(S, L, DFF, T=512):
    assert S % 2048 == 0 and DFF % 128 == 0
    FC = DFF // 128
    NT = S // T
    NTG = T // 128
    NBLK = S // 128
    nc = bass.Bass("TRN2", target_bir_lowering=False)

    def din(name, shape):
        return nc.dram_tensor(name, list(shape), F32, kind="ExternalInput").ap()

    xT_in = din("xT", [D, S])
    pT = din("pT", [L, PLE, S])
    vecs_d = din("vecs", [128, L * NVL])
    sgn_d = din("sgn", [L, 128, 1024])
    sgb_d = din("sgb", [L, 128, 1024])
    sgwT_d = din("sgwT", [L, 8, 128, 128])
    mask2_d = din("mask2", [128, 256])
    triu_d = din("triu", [128, 128])
    invc_d = din("invc", [128, 64])
    W = {}
    for name, shp in (("ffn1_w1", [L, FC, 128, KC, 128]), ("ffn1_w3", [L, FC, 128, KC, 128]),
                      ("ffn1_w2", [L, KC, 128, FC, 128]),
                      ("w_in", [L, IN_COLS // 128, 128, KC, 128]), ("rg_wa", [L, 16, 64, 64]), ("rg_wx", [L, 16, 64, 64]),
                      ("pool_w", [L, 4, 256, 256]), ("w_branch", [L, 4, KC, 128, 8, 128]), ("w_out", [L, KC, 128, KC, 128]),
                      ("ffn2_w1", [L, FC, 128, KC, 128]), ("ffn2_w3", [L, FC, 128, KC, 128]),
                      ("ffn2_w2", [L, KC, 128, FC, 128]),
                      ("ple_gate_w", [L, KC, 128, KC, 128]), ("ple_proj", [L, KC, 128, 2, 128])):
        W[name] = din(name, shp)
    outT = nc.dram_tensor("outT", [D, S], F32, kind="ExternalOutput").ap()
    xs = nc.dram_tensor("xs", [D, S], F32).ap()
    qTd = nc.dram_tensor("qTd", [3072, S], BF16).ap()
    kTd = nc.dram_tensor("kTd", [1024, S], BF16).ap()
    vtm = nc.dram_tensor("vtm", [S, 1024], BF16).ap()
    yT = [nc.dram_tensor(f"y{b}T", [1024, S], BF16).ap() for b in range(4)]

    stack = contextlib.ExitStack()
    with stack:
        ARENA = 212480
        arena = stack.enter_context(nc.sbuf_tensor("arena", [128, ARENA], U8))
        psb = [stack.enter_context(nc.psum_tensor(f"ps{b}", [128, 512], F32)) for b in range(8)]
        tr = Tr(nc, stack)
        off = [0]
        cnt = [0]
        arena_addr = nc.lookup_mloc(arena).addr if SPLIT else 0

        def carve(shape, dt, at=None):
            esz = 4 if dt == F32 else 2
            n = int(np.prod(shape)) * esz
            if at is None:
                o = off[0]
                off[0] = (o + n + 31) // 32 * 32
            else:
                o = at[0]
                at[0] = (o + n + 31) // 32 * 32
            assert o + n <= ARENA, (o, n)
            if SPLIT:
                cnt[0] += 1
                hnd = nc.alloc_sbuf_tensor_at(f"t{cnt[0]}", [128, n // esz], dt, offset=arena_addr + o)
                ap = hnd[:, :]
            else:
                ap = arena[:, o:o + n].bitcast(dt)
            if len(shape) == 2:
                return ap.rearrange("p (a b) -> p a b", b=shape[1])
            if len(shape) == 3:
                return ap.rearrange("p (a b c) -> p a b c", b=shape[1], c=shape[2])
            return ap

        ones_f = carve([128], F32)
        ones_b = carve([128], BF16)
        epsT = carve([1], F32)
        vecs = carve([L * NVL], F32)
        c1t = carve([L * 8], F32)
        c2t = carve([L * 8], F32)
        mask2 = carve([256], BF16)
        triu = carve([128], F32)
        invc = carve([64], F32)
        WBE = max(FC * 128, 4096)
        wA = Rot("wA", [carve([KC, 128], BF16) for _ in range(3)])
        wBr = [carve([WBE], BF16) for _ in range(2)]
        wB = Rot("wB", wBr)
        base = off[0]
        psr = Rot("ps", [psb[b][:, :] for b in range(8)])

        def vcol(l, o):
            return vecs[:, l * NVL + o: l * NVL + o + 1]

        def dma(eng, out, in_, reads=(), writes=()):
            return tr.op(eng, lambda e: e.dma_start(out=out, in_=in_), reads, writes, dma=True)

        def act(out, in_, func, reads, writes, scale=1.0, bias=None, accum_out=None):
            kw = {}
            if bias is not None:
                kw["bias"] = bias
            if accum_out is not None:
                kw["accum_out"] = accum_out
            return tr.op("act", lambda e: e.activation(out=out, in_=in_, func=func, scale=scale, **kw), reads, writes)

        def tt(eng, out, in0, in1, op, reads, writes):
            return tr.op(eng, lambda e: e.tensor_tensor(out=out, in0=in0, in1=in1, op=op), reads, writes)

        def stt(out, in0, scalar, in1, op0, op1, reads, writes, eng="dve"):
            return tr.op(eng, lambda e: e.scalar_tensor_tensor(out=out, in0=in0, scalar=scalar, in1=in1, op0=op0, op1=op1),
                         reads, writes)

        def mm(out, lhsT, rhs, start, stop, reads, writes):
            return tr.op("pe", lambda e: e.matmul(out, lhsT, rhs, start=start, stop=stop), reads, writes)

        tr.op("dve", lambda e: e.memset(ones_f, 1.0), (), ["ones_f"])
        tr.op("dve", lambda e: e.memset(ones_b, 1.0), (), ["ones_b"])
        tr.op("dve", lambda e: e.memset(epsT, EPS), (), ["eps"])
        dma("sp", vecs, vecs_d[:, :], (), ["vecs"])
        dma("pool", mask2, mask2_d[:, :], (), ["mask2"])
        dma("sp", triu, triu_d[:, :], (), ["triu"])
        dma("sp", invc, invc_d[:, :], (), ["invc"])
        for l in range(L):
            lam = vecs[:, l * NVL + V_LAM: l * NVL + V_LAM + 8]
            c1 = c1t[:, l * 8:(l + 1) * 8]
            c2 = c2t[:, l * 8:(l + 1) * 8]
            act(c1, lam, AF.Exp, ["vecs"], ["c1"], scale=-1.0)
            act(c1, c1, AF.Ln, ["c1"], ["c1"], scale=1.0, bias=1.0)
            tr.op("dve", lambda e, c1=c1, c2=c2: e.tensor_scalar_mul(out=c2, in0=c1, scalar1=-16.0), ["c1"], ["c2"])
            tr.op("dve", lambda e, c1=c1: e.tensor_scalar_mul(out=c1, in0=c1, scalar1=-8.0), ["c1", "c2"], ["c1"])

        o13 = [base]
        xt = carve([KC, T], F32, o13)
        hT = carve([KC, T], BF16, o13)
        big = carve([max(FC, 32), T], BF16, o13)
        sqr = Rot("sq", [carve([T], F32, o13) for _ in range(2)])
        sar = Rot("sa", [carve([T], F32, o13) for _ in range(2)])
        rstd = carve([T], F32, o13)
        base13 = o13[0]
        o1 = [base13]
        xcvr = Rot("xcv", [carve([3 + T], F32, o1) for _ in range(1)])
        halo_a = carve([8, 3], F32, o1)
        state = carve([8], F32, o1)
        xc = carve([T], F32, o1)
        xcb = big[:, 10, :]
        r_t = carve([T], F32, o1)
        ig_t = carve([T], F32, o1)
        a_t = carve([T], F32, o1)
        m_t = carve([T], F32, o1)
        hs_t = r_t
        gg_t = m_t
        yar = Rot("ya", [big[:, 11, :], big[:, 12, :]], [("big", 11), ("big", 12)])
        wabd = carve([8, 128], BF16, o1)
        wxbd = carve([8, 128], BF16, o1)
        xbv = carve([15 + T], F32, o1)
        halo_b = carve([8, 15], F32, o1)
        pP = carve([15 + T], F32, o1)
        pQ = carve([15 + T], F32, o1)
        pooled = big[:, 8:10, :]
        tmp15 = carve([16], F32, o1)
        poolw = carve([4, 2, 256], BF16, o1)
        ybr = Rot("yb", [big[:, 13, :], big[:, 14, :]], [("big", 13), ("big", 14)])
        qbr = Rot("qb", [big[:, 15, :], big[:, 16, :]], [("big", 15), ("big", 16)])
        rs_t = rstd
        vbr = Rot("vb", [carve([256], BF16, o1) for _ in range(2)])
        uT = big[:, 0:8, :]
        gvf = carve([1024], F32, o1)
        ss1 = carve([1], F32, o1)
        vvn = carve([1024], BF16, o1)
        junk = vvn
        sgn_t = carve([1024], F32, o1)
        sgb_t = carve([8, 128], F32, o1)
        wsT = carve([8, 128], BF16, o1)
        wsfr = Rot("wsf", [carve([128], F32, o1) for _ in range(2)])
        tmpd = Rot("tmpd", [carve([128], F32, o1) for _ in range(2)])
        ydbr = Rot("ydb", [carve([8, 128], BF16, o1) for _ in range(2)])
        o3 = [base13]
        mergedT = carve([KC, T], BF16, o3)
        sgr = Rot("sg", [carve([T], F32, o3) for _ in range(2)])
        tmr = Rot("tm", [carve([T], F32, o3) for _ in range(2)])
        acc = carve([T], F32, o3)
        ptile = carve([2, T], BF16, o3)
        o2 = [base]
        kTh = carve([S], BF16, o2)
        qTh = [carve([S], BF16, o2) for _ in range(3)]
        Vt = [carve([NBLK, 128], BF16, o2) for _ in range(3)]
        num_acc = carve([2048], F32, o2)
        den_acc = carve([2048], F32, o2)
        pbr = Rot("pb", [carve([256], BF16, o2) for _ in range(3)])
        ycb = carve([2048], BF16, o2)

        def marker(e):
            return e.memset(epsT, EPS)

        def norm_to_hT(l, vo):
            pss, pk = psr.next()
            for kc in range(KC):
                sq, sk = sqr.next()
                act(sq, xt[:, kc, :], AF.Square, [("xt", kc)], [sk])
                mm(pss, ones_f, sq, kc == 0, kc == KC - 1, [sk, "ones_f"], [pk])
            act(rstd, pss, AF.Sqrt, [pk, "eps"], ["rstd"], scale=1.0 / D, bias=epsT)
            tr.op("dve", lambda e: e.reciprocal(out=rstd, in_=rstd), ["rstd"], ["rstd"])
            for kc in range(KC):
                stt(hT[:, kc, :], xt[:, kc, :], vcol(l, vo + kc), rstd, ALU.mult, ALU.mult,
                    [("xt", kc), "rstd", "vecs"], [("hT", kc)])

        def lin(wap, nk, rhs_fn, rhs_keys):
            wt, wk = wA.next()
            dma("pool", wt[:, 0:nk, :], wap, (), [wk])
            pb, pk = psr.next()
            for kc in range(nk):
                mm(pb, wt[:, kc, :], rhs_fn(kc), kc == 0, kc == nk - 1, [wk, rhs_keys(kc)], [pk])
            return pb, pk

        def proj(l, col):
            return lin(W["w_in"][l, col // 128], KC, lambda kc: hT[:, kc, :], lambda kc: ("hT", kc))

        def ffn(l, pre, vo):
            norm_to_hT(l, vo)
            w1, w3, w2 = W[pre + "_w1"], W[pre + "_w3"], W[pre + "_w2"]
            for fc in range(FC):
                pa, pak = lin(w1[l, fc], KC, lambda kc: hT[:, kc, :], lambda kc: ("hT", kc))
                pb, pbk = lin(w3[l, fc], KC, lambda kc: hT[:, kc, :], lambda kc: ("hT", kc))
                sa, sk = sar.next()
                act(sa, pa, AF.Silu, [pak], [sk])
                tt("dve", big[:, fc, :], sa, pb, ALU.mult, [sk, pbk], [("big", fc)])
            for dc in range(KC):
                wt, wk = wB.next()
                wv = wt[:, 0:FC * 128].rearrange("p (f n) -> p f n", n=128)
                dma("pool", wv, w2[l, dc], (), [wk])
                po, pk = psr.next()
                for fc in range(FC):
                    mm(po, wv[:, fc, :], big[:, fc, :], fc == 0, fc == FC - 1, [wk, ("big", fc)], [pk])
                stt(xt[:, dc, :], po, 0.5, xt[:, dc, :], ALU.mult, ALU.add, [pk, ("xt", dc)], [("xt", dc)])

        def load_xt(src, c0):
            dma("sp", xt, src[:, c0:c0 + T].rearrange("(kc p) t -> p kc t", p=128), (), [("xt", kc) for kc in range(KC)])

        def store_xt(dst, c0):
            dma("sp", dst[:, c0:c0 + T].rearrange("(kc p) t -> p kc t", p=128), xt, [("xt", kc) for kc in range(KC)], ())

        def qk_norm(pq, pqk, gcol, dst_ap):
            sq, sk = sqr.next()
            act(sq, pq, AF.Square, [pqk], [sk])
            p2, p2k = psr.next()
            mm(p2, ones_f, sq, True, True, [sk, "ones_f"], [p2k])
            act(rs_t, p2, AF.Sqrt, [p2k, "eps"], ["rstd"], scale=1.0 / 128, bias=epsT)
            tr.op("dve", lambda e: e.reciprocal(out=rs_t, in_=rs_t), ["rstd"], ["rstd"])
            qb, qk = qbr.next()
            stt(qb, pq, gcol, rs_t, ALU.mult, ALU.mult, [pqk, "rstd", "vecs"], [qk])
            dma("sp", dst_ap, qb, [qk], ())

        def layer_setup(l):
            tr.op("dve", lambda e: e.memset(wabd, 0.0), (), ["wabd"])
            tr.op("dve", lambda e: e.memset(wxbd, 0.0), (), ["wxbd"])
            for hh in range(16):
                ci, half = hh // 2, hh % 2
                dma("pool", wabd[half * 64:(half + 1) * 64, ci, half * 64:(half + 1) * 64], W["rg_wa"][l, hh], (), ["wabd"])
                dma("pool", wxbd[half * 64:(half + 1) * 64, ci, half * 64:(half + 1) * 64], W["rg_wx"][l, hh], (), ["wxbd"])
            for g in range(4):
                dma("pool", poolw[:, g, :, :], W["pool_w"][l, g].rearrange("(c p) d -> p c d", p=128), (), ["poolw"])
            dma("sp", sgn_t, sgn_d[l], (), ["sgn"])
            dma("sp", sgb_t, sgb_d[l].rearrange("p (g t) -> p g t", t=128), (), ["sgb"])
            for g in range(8):
                wf, wfk = wsfr.next()
                dma("sp", wf, sgwT_d[l, g], (), [wfk])
                tt("dve", wsT[:, g, :], wf, triu, ALU.mult, [wfk, "triu"], ["wsT"])

        def mixer_A(l, t, c0):
            for ci in range(8):
                px, pxk = proj(l, O_XA + ci * 128)
                pg, pgk = proj(l, O_GA + ci * 128)
                xcv, xk = xcvr.next()
                if t == 0:
                    tr.op("dve", lambda e, xcv=xcv: e.memset(xcv[:, 0:3], 0.0), (), [xk])
                else:
                    tr.op("dve", lambda e, xcv=xcv, ci=ci: e.tensor_copy(out=xcv[:, 0:3], in_=halo_a[:, ci, :]),
                          [("haloa", ci)], [xk])
                act(xcv[:, 3:3 + T], px, AF.Copy, [pxk], [xk])
                tr.op("dve", lambda e, xcv=xcv, ci=ci: e.tensor_copy(out=halo_a[:, ci, :], in_=xcv[:, T:T + 3]),
                      [xk], [("haloa", ci)])
                w0, b0 = vcol(l, V_CW + ci), vcol(l, V_CB + ci)
                tr.op("dve", lambda e, xcv=xcv, w0=w0, b0=b0: e.tensor_scalar(
                    out=xc, in0=xcv[:, 3:3 + T], scalar1=w0, scalar2=b0, op0=ALU.mult, op1=ALU.add), [xk, "vecs"], ["xc"])
                for j in range(1, 4):
                    stt(xc, xcv[:, 3 - j:3 - j + T], vcol(l, V_CW + j * 8 + ci), xc, ALU.mult, ALU.add,
                        [xk, "xc", "vecs"], ["xc"])
                act(xcb, xc, AF.Copy, ["xc"], [("big", 10)])
                pr, prk = psr.next()
                mm(pr, wabd[:, ci, :], xcb, True, True, ["wabd", ("big", 10)], [prk])
                pi, pik = psr.next()
                mm(pi, wxbd[:, ci, :], xcb, True, True, ["wxbd", ("big", 10)], [pik])
                act(r_t, pr, AF.Sigmoid, [prk, "vecs"], ["r"], bias=vcol(l, V_BA + ci))
                act(ig_t, pi, AF.Sigmoid, [pik, "vecs"], ["ig"], bias=vcol(l, V_BX + ci))
                act(a_t, r_t, AF.Exp, ["r", "c1"], ["a"], scale=c1t[:, l * 8 + ci:l * 8 + ci + 1])
                act(m_t, r_t, AF.Exp, ["r", "c2"], ["m"], scale=c2t[:, l * 8 + ci:l * 8 + ci + 1])
                act(m_t, m_t, AF.Sqrt, ["m"], ["m"], scale=-1.0, bias=1.0)
                tt("dve", ig_t, ig_t, m_t, ALU.mult, ["ig", "m"], ["ig"])
                tt("dve", ig_t, ig_t, xc, ALU.mult, ["ig", "xc"], ["ig"])
                if t == 0:
                    tr.op("dve", lambda e, ci=ci: e.memset(state[:, ci:ci + 1], 0.0), (), [("st", ci)])
                tr.op("dve", lambda e, ci=ci: e.tensor_tensor_scan(
                    out=hs_t, data0=a_t, data1=ig_t, initial=state[:, ci:ci + 1], op0=ALU.mult, op1=ALU.add),
                    ["a", "ig", ("st", ci)], ["r"])
                tr.op("dve", lambda e, ci=ci: e.tensor_copy(out=state[:, ci:ci + 1], in_=hs_t[:, T - 1:T]),
                      ["r"], [("st", ci)])
                act(gg_t, pg, AF.Gelu, [pgk], ["m"])
                ya, yk = yar.next()
                tt("dve", ya, hs_t, gg_t, ALU.mult, ["r", "m"], [yk])
                dma("sp", yT[0][ci * 128:(ci + 1) * 128, c0:c0 + T], ya, [yk], ())

        def mixer_B(l, t, c0):
            for g in range(4):
                win = 2 << g
                for j in range(2):
                    ch = 2 * g + j
                    pb_, pbk = proj(l, O_XB + ch * 128)
                    if t == 0:
                        tr.op("dve", lambda e: e.memset(xbv[:, 0:15], 0.0), (), ["xbv"])
                    else:
                        tr.op("dve", lambda e, ch=ch: e.tensor_copy(out=xbv[:, 0:15], in_=halo_b[:, ch, :]),
                              [("halob", ch)], ["xbv"])
                    act(xbv[:, 15:15 + T], pb_, AF.Copy, [pbk], ["xbv"])
                    tr.op("dve", lambda e, ch=ch: e.tensor_copy(out=halo_b[:, ch, :], in_=xbv[:, T:T + 15]),
                          ["xbv"], [("halob", ch)])
                    srcb, sk_ = xbv, "xbv"
                    bufs = [(pP, "pP"), (pQ, "pQ")]
                    for k in range(1, g + 2):
                        lo, sh = (1 << k) - 1, 1 << (k - 1)
                        dstb, dk_ = bufs[k % 2]
                        tt("dve", dstb[:, lo:], srcb[:, lo:], srcb[:, lo - sh:15 + T - sh], ALU.add, [sk_], [dk_])
                        srcb, sk_ = dstb, dk_
                    stt(pooled[:, j, :], srcb[:, 15:], 1.0 / win, xbv[:, 15:], ALU.mult, ALU.subtract,
                        [sk_, "xbv"], [("big", 8 + j)])
                    if t == 0:
                        w1_ = win - 1
                        tt("dve", tmp15[:, 0:w1_], srcb[:, 15:15 + w1_], invc[:, g * 16:g * 16 + w1_], ALU.mult,
                           [sk_, "invc"], ["tmp15"])
                        tt("dve", pooled[:, j, 0:w1_], tmp15[:, 0:w1_], xbv[:, 15:15 + w1_], ALU.subtract,
                           ["tmp15", "xbv"], [("big", 8 + j)])
                for dch in range(2):
                    pp, ppk = psr.next()
                    for cch in range(2):
                        mm(pp, poolw[:, g, cch, dch * 128:(dch + 1) * 128], pooled[:, cch, :], cch == 0, cch == 1,
                           ["poolw", ("big", 8 + cch)], [ppk])
                    yb, ybk = ybr.next()
                    act(yb, pp, AF.Identity, [ppk, "vecs"], [ybk], scale=vcol(l, V_PS + 2 * g + dch))
                    ch = 2 * g + dch
                    dma("sp", yT[1][ch * 128:(ch + 1) * 128, c0:c0 + T], yb, [ybk], ())

        def tokmajor_slab(l, col):
            wt, wk = wB.next()
            wv = wt[:, 0:4096].rearrange("p (k n) -> p k n", n=256)
            dma("pool", wv[:, :, 0:128], W["w_in"][l, col // 128], (), [wk])
            dma("pool", wv[:, :, 128:256], W["w_in"][l, col // 128 + 1], (), [wk])
            return wv, wk

        def mixer_C_inputs(l, t, c0):
            for hq in range(24):
                pq, pqk = proj(l, O_Q + hq * 128)
                qk_norm(pq, pqk, vcol(l, V_QG), qTd[hq * 128:(hq + 1) * 128, c0:c0 + T])
            for hk in range(8):
                pq, pqk = proj(l, O_K + hk * 128)
                qk_norm(pq, pqk, vcol(l, V_KG), kTd[hk * 128:(hk + 1) * 128, c0:c0 + T])
            for qv in range(4):
                wv, wk = tokmajor_slab(l, O_V + qv * 256)
                for tg in range(NTG):
                    pv, pvk = psr.next()
                    for kc in range(KC):
                        mm(pv[:, 0:256], hT[:, kc, tg * 128:(tg + 1) * 128], wv[:, kc, :], kc == 0, kc == KC - 1,
                           [wk, ("hT", kc)], [pvk])
                    vb, vk = vbr.next()
                    act(vb, pv[:, 0:256], AF.Copy, [pvk], [vk])
                    dma("sp", vtm[c0 + tg * 128:c0 + (tg + 1) * 128, qv * 256:(qv + 1) * 256], vb, [vk], ())

        def mixer_D(l, t, c0):
            for g in range(8):
                pu, puk = proj(l, O_U + g * 128)
                act(uT[:, g, :], pu, AF.Gelu, [puk], [("big", g)])
            for tg in range(NTG):
                pva, pvak = psr.next()
                pvb, pvbk = psr.next()
                for qv in range(4):
                    wv, wk = tokmajor_slab(l, O_VV + qv * 256)
                    pdst, pdk = (pva, pvak) if qv < 2 else (pvb, pvbk)
                    for kc in range(KC):
                        mm(pdst[:, (qv % 2) * 256:(qv % 2 + 1) * 256], hT[:, kc, tg * 128:(tg + 1) * 128], wv[:, kc, :],
                           kc == 0, kc == KC - 1, [wk, ("hT", kc)], [pdk])
                act(gvf[:, 0:512], pva, AF.Gelu, [pvak], ["gvf"])
                act(gvf[:, 512:1024], pvb, AF.Gelu, [pvbk], ["gvf"])
                act(junk, gvf, AF.Square, ["gvf"], ["vvn", "ss1"], accum_out=ss1)
                act(ss1, ss1, AF.Sqrt, ["ss1", "eps"], ["ss1"], scale=1.0 / 1024, bias=epsT)
                tr.op("dve", lambda e: e.reciprocal(out=ss1, in_=ss1), ["ss1"], ["ss1"])
                stt(vvn, gvf, ss1, sgn_t, ALU.mult, ALU.mult, ["gvf", "ss1", "sgn"], ["vvn"])
                ydb, ydk = ydbr.next()
                for g in range(8):
                    pm, pmk = psr.next()
                    mm(pm[:, 0:128], vvn[:, g * 128:(g + 1) * 128], wsT[:, g, :], True, True, ["vvn", "wsT"], [pmk])
                    td, tk = tmpd.next()
                    tt("dve", td, pm[:, 0:128], sgb_t[:, g, :], ALU.add, [pmk, "sgb"], [tk])
                    tt("dve", ydb[:, g, :], td, uT[:, g, tg * 128:(tg + 1) * 128], ALU.mult, [tk, ("big", g)], [ydk])
                dma("sp", yT[3][:, c0 + tg * 128:c0 + (tg + 1) * 128].rearrange("(g p) t -> p g t", p=128), ydb, [ydk], ())

        def attention(l):
            scale = 128.0 ** -0.5
            for h in range(8):
                dma("sp", kTh, kTd[h * 128:(h + 1) * 128, :], (), ["kTh"])
                for g in range(3):
                    dma("sp", qTh[g], qTd[(g * 8 + h) * 128:(g * 8 + h + 1) * 128, :], (), [("qTh", g)])
                for g, dil in enumerate((1, 4, 16)):
                    span = 128 * dil
                    for j in range(S // span):
                        dma("sp", Vt[g][:, j * dil:(j + 1) * dil, :],
                            vtm[j * span:(j + 1) * span, h * 128:(h + 1) * 128].rearrange("(i r) d -> i r d", r=dil),
                            (), [("Vt", g, j)])
                for w in range(S // 2048):
                    for g, dil in enumerate((1, 4, 16)):
                        span = 128 * dil
                        ext = 127 * dil + 1
                        for j in range(w * 2048 // span, (w + 1) * 2048 // span):
                            for r in range(dil):
                                b0 = j * span + r
                                qsl = qTh[g][:, b0:b0 + ext:dil]
                                pss_, psk = psr.next()
                                lo = 0 if j > 0 else 128
                                if j > 0:
                                    mm(pss_[:, 0:128], kTh[:, b0 - span:b0 - span + ext:dil], qsl, True, True,
                                       ["kTh", ("qTh", g)], [psk])
                                mm(pss_[:, 128:256], kTh[:, b0:b0 + ext:dil], qsl, True, True, ["kTh", ("qTh", g)], [psk])
                                pb_, pbk = pbr.next()
                                act(pb_[:, lo:256], pss_[:, lo:256], AF.Exp, [psk], [pbk], scale=scale)
                                tt("dve", pb_[:, lo:256], pb_[:, lo:256], mask2[:, lo:256], ALU.mult, [pbk, "mask2"], [pbk])
                                pn, pnk = psr.next()
                                pd, pdk = psr.next()
                                blk = j * dil + r
                                if j > 0:
                                    mm(pn[:, 0:128], Vt[g][:, blk - dil, :], pb_[:, 0:128], True, False,
                                       [("Vt", g, j - 1), pbk], [pnk])
                                    mm(pd[:, 0:128], ones_b, pb_[:, 0:128], True, False, ["ones_b", pbk], [pdk])
                                mm(pn[:, 0:128], Vt[g][:, blk, :], pb_[:, 128:256], j == 0, True, [("Vt", g, j), pbk], [pnk])
                                mm(pd[:, 0:128], ones_b, pb_[:, 128:256], j == 0, True, ["ones_b", pbk], [pdk])
                                wb0 = b0 - w * 2048
                                nsl = num_acc[:, wb0:wb0 + ext:dil]
                                dsl = den_acc[:, wb0:wb0 + ext:dil]
                                if g == 0:
                                    act(nsl, pn[:, 0:128], AF.Copy, [pnk], ["num"])
                                    act(dsl, pd[:, 0:128], AF.Copy, [pdk], ["den"])
                                else:
                                    tt("dve", nsl, pn[:, 0:128], nsl, ALU.add, [pnk, "num"], ["num"])
                                    tt("dve", dsl, pd[:, 0:128], dsl, ALU.add, [pdk, "den"], ["den"])
                    tr.op("dve", lambda e: e.reciprocal(out=den_acc, in_=den_acc), ["den"], ["den"])
                    tt("dve", ycb, num_acc, den_acc, ALU.mult, ["num", "den"], ["ycb"])
                    dma("sp", yT[2][h * 128:(h + 1) * 128, w * 2048:(w + 1) * 2048], ycb, ["ycb"], ())

        def pass3_tile(l, t, c0, dst):
            load_xt(xs, c0)
            norm_to_hT(l, V_MIX)
            for b in range(4):
                dma("sp", big[:, 8 * b:8 * b + 8, :], yT[b][:, c0:c0 + T].rearrange("(kc p) t -> p kc t", p=128),
                    (), [("big", 8 * b + kc) for kc in range(8)])
            for dc in range(KC):
                for b in range(4):
                    pg, pgk = proj(l, O_G + b * 2048 + dc * 128)
                    sg, sgk = sgr.next()
                    act(sg, pg, AF.Sigmoid, [pgk, "vecs"], [sgk], bias=vcol(l, V_BG + b * 16 + dc))
                    py, pyk = lin(W["w_branch"][l, b, dc], 8,
                                  lambda kc, b=b: big[:, 8 * b + kc, :], lambda kc, b=b: ("big", 8 * b + kc))
                    if b == 0:
                        tt("dve", acc, py, sg, ALU.mult, [pyk, sgk], ["acc"])
                    else:
                        tm, tmk = tmr.next()
                        tt("dve", tm, py, sg, ALU.mult, [pyk, sgk], [tmk])
                        if b < 3:
                            tt("pool", acc, acc, tm, ALU.add, ["acc", tmk], ["acc"])
                        else:
                            tt("pool", mergedT[:, dc, :], acc, tm, ALU.add, ["acc", tmk], [("mg", dc)])
            for dc in range(KC):
                po, pok = lin(W["w_out"][l, dc], KC,
                              lambda kc: mergedT[:, kc, :], lambda kc: ("mg", kc))
                tt("dve", xt[:, dc, :], po, xt[:, dc, :], ALU.add, [pok, ("xt", dc)], [("xt", dc)])
            ffn(l, "ffn2", V_F2)
            norm_to_hT(l, V_PLE)
            dma("pool", ptile, pT[l, :, c0:c0 + T].rearrange("(kc p) t -> p kc t", p=128), (), ["ptile"])
            for dc in range(KC):
                pg, pgk = lin(W["ple_gate_w"][l, dc], KC,
                              lambda kc: hT[:, kc, :], lambda kc: ("hT", kc))
                sg, sgk = sgr.next()
                act(sg, pg, AF.Sigmoid, [pgk], [sgk])
                pp, ppk = lin(W["ple_proj"][l, dc], 2,
                              lambda kc: ptile[:, kc, :], lambda kc: "ptile")
                tm, tmk = tmr.next()
                tt("dve", tm, pp, sg, ALU.mult, [ppk, sgk], [tmk])
                tt("pool", xt[:, dc, :], xt[:, dc, :], tm, ALU.add, [("xt", dc), tmk], [("xt", dc)])
            store_xt(dst, c0)

        for l in range(L):
            src = xT_in if l == 0 else xs
            layer_setup(l)
            for t in range(NT):
                c0 = t * T
                load_xt(src, c0)
                ffn(l, "ffn1", V_F1)
                norm_to_hT(l, V_MIX)
                mixer_A(l, t, c0)
                mixer_B(l, t, c0)
                mixer_C_inputs(l, t, c0)
                mixer_D(l, t, c0)
                store_xt(xs, c0)
            tr.barrier(marker)
            attention(l)
            tr.barrier(marker)
            dst = outT if l == L - 1 else xs
            for t in range(NT):
                pass3_tile(l, t, t * T, dst)
            tr.barrier(marker)

        with nc.Block() as block:
            @block.tensor
            def _(e):
                tr.emit("pe", e)

            @block.scalar
            def _(e):
                tr.emit("act", e)

            @block.vector
            def _(e):
                tr.emit("dve", e)

            @block.gpsimd
            def _(e):
                tr.emit("pool", e)

            @block.sync
            def _(e):
                tr.emit("sp", e)
    return nc


def host_consts():
    ki = np.arange(128)[:, None]
    qi = np.arange(128)[None, :]
    mask2 = np.concatenate([(ki >= qi), (ki <= qi)], axis=1).astype(np.float32)
    triu = (ki <= qi).astype(np.float32)
    invc = np.ones((128, 64), np.float32)
    for g, win in enumerate((2, 4, 8, 16)):
        for pos in range(16):
            invc[:, g * 16 + pos] = 1.0 / min(pos + 1, win)
    return mask2, triu, invc


def _cols(v):
    v = np.asarray(v, np.float32).reshape(-1, 128)
    return v.T


def host_layout(inp, L):
    vecs = np.zeros((128, L * NVL), np.float32)
    for l in range(L):
        o = l * NVL
        vecs[:, o + V_F1:o + V_F1 + 16] = _cols(inp["ffn1_norm"][l])
        vecs[:, o + V_MIX:o + V_MIX + 16] = _cols(inp["mix_norm"][l])
        vecs[:, o + V_F2:o + V_F2 + 16] = _cols(inp["ffn2_norm"][l])
        vecs[:, o + V_PLE:o + V_PLE + 16] = _cols(inp["ple_norm"][l])
        for b in range(4):
            vecs[:, o + V_BG + b * 16:o + V_BG + (b + 1) * 16] = _cols(inp["b_gate"][l, b])
        for j in range(4):
            vecs[:, o + V_CW + j * 8:o + V_CW + (j + 1) * 8] = _cols(inp["conv_w"][l, j])
        vecs[:, o + V_CB:o + V_CB + 8] = _cols(inp["conv_b"][l])
        vecs[:, o + V_BA:o + V_BA + 8] = _cols(inp["rg_ba"][l])
        vecs[:, o + V_BX:o + V_BX + 8] = _cols(inp["rg_bx"][l])
        vecs[:, o + V_LAM:o + V_LAM + 8] = _cols(inp["rg_lambda"][l])
        vecs[:, o + V_PS:o + V_PS + 8] = _cols(inp["pool_scale"][l])
        vecs[:, o + V_QG:o + V_QG + 1] = _cols(inp["q_gain"][l])
        vecs[:, o + V_KG:o + V_KG + 1] = _cols(inp["k_gain"][l])
    sgn = np.ascontiguousarray(np.broadcast_to(np.asarray(inp["sg_norm"], np.float32)[:, None, :], (L, 128, 1024)))
    sgb = np.ascontiguousarray(np.broadcast_to(
        np.asarray(inp["sg_b"], np.float32).reshape(L, 1, 1024), (L, 128, 1024)))
    sgwT = np.ascontiguousarray(np.asarray(inp["sg_w"], np.float32).transpose(0, 1, 3, 2))
    return vecs, sgn, sgb, sgwT


def _tile_w(w):
    w = np.asarray(w, np.float32)
    lead = w.shape[:-2]
    K_, N_ = w.shape[-2:]
    w = w.reshape(*lead, K_ // 128, 128, N_ // 128, 128)
    nl = len(lead)
    w = w.transpose(*range(nl), nl + 2, nl + 1, nl, nl + 3)
    return np.ascontiguousarray(w)


_TILED = ("ffn1_w1", "ffn1_w3", "ffn1_w2", "w_in", "w_branch", "w_out", "ffn2_w1", "ffn2_w3", "ffn2_w2",
          "ple_gate_w", "ple_proj")
_WNAMES = ("ffn1_w1", "ffn1_w3", "ffn1_w2", "w_in", "rg_wa", "rg_wx", "pool_w", "w_branch", "w_out",
           "ffn2_w1", "ffn2_w3", "ffn2_w2", "ple_gate_w", "ple_proj")


def run(inp, trace=False, per_layer=False):
    x = np.asarray(inp["x"], np.float32)
    B, S, _ = x.shape
    L = inp["w_in"].shape[0]
    DFF = inp["ffn1_w1"].shape[2]
    mask2, triu, invc = host_consts()
    p = np.asarray(inp["p"], np.float32)
    xT = [np.ascontiguousarray(x[b].T) for b in range(B)]
    groups = [[l] for l in range(L)] if per_layer else [list(range(L))]
    nc = build_program(S, len(groups[0]), DFF)
    res = None
    for grp in groups:
        sub = {k: np.asarray(v)[grp[0]:grp[-1] + 1] for k, v in inp.items() if k != "x"}
        vecs, sgn, sgb, sgwT = host_layout(sub, len(grp))
        shared = {n: (_tile_w(sub[n]) if n in _TILED else np.ascontiguousarray(np.asarray(sub[n], np.float32)))
                  for n in _WNAMES}
        shared.update(vecs=vecs, sgn=sgn, sgb=sgb, sgwT=sgwT, mask2=mask2, triu=triu, invc=invc)
        in_maps = []
        for b in range(B):
            m = dict(shared)
            m["xT"] = xT[b]
            m["pT"] = np.ascontiguousarray(p[grp[0]:grp[-1] + 1, b].transpose(0, 2, 1))
            in_maps.append(m)
        res = run_bass_kernel_spmd(nc, in_maps, core_ids=list(range(B)), trace=trace)
        xT = [np.ascontiguousarray(res.results[b]["outT"]) for b in range(B)]
    out = np.stack([np.ascontiguousarray(xT[b].T) for b in range(B)], axis=0)
    return out.astype(np.float32), res


def kernel(**inputs):
    out, _ = run(inputs)
    return out
```

```python
import contextlib
import numpy as np
import concourse.bass as bass
import concourse.mybir as mybir
from concourse.bass_utils import run_bass_kernel_spmd

F32 = mybir.dt.float32
BF16 = mybir.dt.bfloat16
U8 = mybir.dt.uint8
AF = mybir.ActivationFunctionType
ALU = mybir.AluOpType

D = 2048
KC = 16
PLE = 256
EPS = 1e-6
IN_COLS = 18432
O_XA, O_GA, O_XB, O_Q, O_K, O_V, O_U, O_VV, O_G = 0, 1024, 2048, 3072, 6144, 7168, 8192, 9216, 10240
NVL = 208
V_F1, V_MIX, V_F2, V_PLE, V_BG, V_CW, V_CB, V_BA, V_BX, V_LAM, V_PS, V_QG, V_KG = (
    0, 16, 32, 48, 64, 128, 160, 168, 176, 184, 192, 200, 201)
SAME_ENGINE_SYNC = True
SPLIT = True


class Tok:
    __slots__ = ("sem", "val", "eng", "dma")

    def __init__(self, sem, val, eng, dma):
        self.sem, self.val, self.eng, self.dma = sem, val, eng, dma


class Tr:
    PERIOD = 24000
    DMA_POOL = 12
    DMA_MAX_USES = 1700

    def __init__(self, nc, stack):
        self.nc, self.stack = nc, stack
        self.engs = ["pe", "act", "dve", "pool", "sp"]
        self.recs = {e: [] for e in self.engs}
        self.count = {e: 0 for e in self.engs}
        self.esems = {e: [] for e in self.engs}
        self.seen = {e: {} for e in self.engs}
        self.last_w = {}
        self.readers = {}
        self.dpool = {e: [] for e in self.engs}
        self.dnext = {e: 0 for e in self.engs}
        self.pending = {}
        self.nsem = 0

    def _newsem(self, name):
        self.nsem += 1
        return self.stack.enter_context(self.nc.semaphore(f"{name}_{self.nsem}"))

    def _need(self, eng, tok, waits):
        if tok is None:
            return
        sid = id(tok.sem)
        if self.seen[eng].get(sid, 0) >= tok.val:
            return
        self.seen[eng][sid] = tok.val
        waits.append((tok.sem, tok.val))

    def op(self, eng, fn, reads=(), writes=(), dma=False):
        waits = []
        deps = []
        for k in reads:
            deps.append(self.last_w.get(k))
        for k in writes:
            deps.append(self.last_w.get(k))
            rd = self.readers.get(k)
            if rd:
                deps.extend(rd.values())
        for tkn in deps:
            if tkn is None:
                continue
            if (not dma) and (not tkn.dma) and tkn.eng == eng:
                if eng == "pe" or not SAME_ENGINE_SYNC:
                    continue
            self._need(eng, tkn, waits)
        if dma:
            pool = self.dpool[eng]
            if len(pool) < self.DMA_POOL:
                pool.append([self._newsem("d" + eng), 0])
            i = self.dnext[eng] % self.DMA_POOL
            self.dnext[eng] += 1
            if pool[i][1] >= self.DMA_MAX_USES:
                pool[i] = [self._newsem("d" + eng), 0]
            sem, uses = pool[i]
            if uses > 0:
                self._need(eng, Tok(sem, 16 * uses, eng, True), waits)
            pool[i][1] = uses + 1
            tok = Tok(sem, 16 * (uses + 1), eng, True)
            inc = 16
            self.pending[id(sem)] = tok
        else:
            n = self.count[eng]
            self.count[eng] = n + 1
            ep = n // self.PERIOD
            if ep >= len(self.esems[eng]):
                self.esems[eng].append(self._newsem("e" + eng))
            sem = self.esems[eng][ep]
            tok = Tok(sem, n % self.PERIOD + 1, eng, False)
            inc = 1
        for k in writes:
            self.last_w[k] = tok
            self.readers[k] = {}
        for k in reads:
            if k in writes:
                continue
            rd = self.readers.setdefault(k, {})
            rd[(eng, id(tok.sem)) if dma else eng] = tok
        self.recs[eng].append((waits, fn, (tok.sem, inc)))
        return tok

    def latest(self, eng):
        n = self.count[eng]
        if n == 0:
            return None
        n -= 1
        return Tok(self.esems[eng][n // self.PERIOD], n % self.PERIOD + 1, eng, False)

    def barrier(self, marker_fn):
        waits = []
        for e in self.engs:
            if e != "dve":
                self._need("dve", self.latest(e), waits)
        for tkn in self.pending.values():
            self._need("dve", tkn, waits)
        self.pending = {}
        self.recs["dve"].append((waits, None, None))
        self.last_w, self.readers = {}, {}
        b = self.op("dve", marker_fn)
        for e in self.engs:
            if e != "dve":
                w = []
                self._need(e, b, w)
                self.recs[e].append((w, None, None))

    def emit(self, eng, e):
        for waits, fn, sig in self.recs[eng]:
            for sem, val in waits:
                e.wait_ge(sem, val)
            if fn is not None:
                ins = fn(e)
                ins.then_inc(sig[0], sig[1])


class Rot:
    def __init__(self, name, aps, keys=None):
        self.name, self.aps, self.i = name, aps, 0
        self.keys = keys if keys is not None else [(name, i) for i in range(len(aps))]

    def next(self):
        i = self.i % len(self.aps)
        self.i += 1
        return self.aps[i], self.keys[i]


def build_program(S, L, DFF, T=512):
    assert S % 2048 == 0 and DFF % 128 == 0
    FC = DFF // 128
    NT = S // T
    NTG = T // 128
    NBLK = S // 128
    nc = bass.Bass("TRN2", target_bir_lowering=False)

    def din(name, shape):
        return nc.dram_tensor(name, list(shape), F32, kind="ExternalInput").ap()

    xT_in = din("xT", [D, S])
    pT = din("pT", [L, PLE, S])
    vecs_d = din("vecs", [128, L * NVL])
    sgn_d = din("sgn", [L, 128, 1024])
    sgb_d = din("sgb", [L, 128, 1024])
    sgwT_d = din("sgwT", [L, 8, 128, 128])
    mask2_d = din("mask2", [128, 256])
    triu_d = din("triu", [128, 128])
    invc_d = din("invc", [128, 64])
    W = {}
    Wf = {}
    WSH = {"ffn1_w1": (FC, KC * 128), "ffn1_w3": (FC, KC * 128), "ffn1_w2": (KC, FC * 128),
           "w_in": (IN_COLS // 128, KC * 128), "w_branch": (4 * KC, 8 * 128), "w_out": (KC, KC * 128),
           "ffn2_w1": (FC, KC * 128), "ffn2_w3": (FC, KC * 128), "ffn2_w2": (KC, FC * 128),
           "ple_gate_w": (KC, KC * 128), "ple_proj": (KC, 2 * 128)}
    for name, (C_, E_) in WSH.items():
        Wf[name] = din(name, [L, C_, 128, E_])
        W[name] = [nc.dram_tensor(f"{name}_b{i}", [C_, 128, E_], BF16).ap() for i in range(2)]
    for name, shp in (("rg_wa", [L, 16, 64, 64]), ("rg_wx", [L, 16, 64, 64]), ("pool_w", [L, 4, 256, 256])):
        W[name] = din(name, shp)

    def wv3(name, l, c):
        return W[name][l % 2][c].rearrange("p (k n) -> p k n", n=128)

    outT = nc.dram_tensor("outT", [D, S], F32, kind="ExternalOutput").ap()
    xs = nc.dram_tensor("xs", [D, S], F32).ap()
    qTd = nc.dram_tensor("qTd", [3072, S], BF16).ap()
    kTd = nc.dram_tensor("kTd", [1024, S], BF16).ap()
    vtm = nc.dram_tensor("vtm", [S, 1024], BF16).ap()
    yT = [nc.dram_tensor(f"y{b}T", [1024, S], BF16).ap() for b in range(4)]

    stack = contextlib.ExitStack()
    with stack:
        ARENA = 212480
        arena = stack.enter_context(nc.sbuf_tensor("arena", [128, ARENA], U8))
        psb = [stack.enter_context(nc.psum_tensor(f"ps{b}", [128, 512], F32)) for b in range(8)]
        tr = Tr(nc, stack)
        off = [0]
        cnt = [0]
        arena_addr = nc.lookup_mloc(arena).addr if SPLIT else 0

        def carve(shape, dt, at=None):
            esz = 4 if dt == F32 else 2
            n = int(np.prod(shape)) * esz
            if at is None:
                o = off[0]
                off[0] = (o + n + 31) // 32 * 32
            else:
                o = at[0]
                at[0] = (o + n + 31) // 32 * 32
            assert o + n <= ARENA, (o, n)
            if SPLIT:
                cnt[0] += 1
                hnd = nc.alloc_sbuf_tensor_at(f"t{cnt[0]}", [128, n // esz], dt, offset=arena_addr + o)
                ap = hnd[:, :]
            else:
                ap = arena[:, o:o + n].bitcast(dt)
            if len(shape) == 2:
                return ap.rearrange("p (a b) -> p a b", b=shape[1])
            if len(shape) == 3:
                return ap.rearrange("p (a b c) -> p a b c", b=shape[1], c=shape[2])
            return ap

        ones_f = carve([128], F32)
        ones_b = carve([128], BF16)
        epsT = carve([1], F32)
        vecs = carve([L * NVL], F32)
        c1t = carve([L * 8], F32)
        c2t = carve([L * 8], F32)
        mask2 = carve([256], BF16)
        triu = carve([128], F32)
        invc = carve([64], F32)
        WBE = max(FC * 128, 4096)
        wA = Rot("wA", [carve([KC, 128], BF16) for _ in range(4)])
        wBr = [carve([WBE], BF16) for _ in range(3)]
        wB = Rot("wB", wBr)
        base = off[0]
        psr = Rot("ps", [psb[b][:, :] for b in range(8)])

        def vcol(l, o):
            return vecs[:, l * NVL + o: l * NVL + o + 1]

        def dma(eng, out, in_, reads=(), writes=()):
            return tr.op(eng, lambda e: e.dma_start(out=out, in_=in_), reads, writes, dma=True)

        def act(out, in_, func, reads, writes, scale=1.0, bias=None, accum_out=None):
            kw = {}
            if bias is not None:
                kw["bias"] = bias
            if accum_out is not None:
                kw["accum_out"] = accum_out
            return tr.op("act", lambda e: e.activation(out=out, in_=in_, func=func, scale=scale, **kw), reads, writes)

        def tt(eng, out, in0, in1, op, reads, writes):
            return tr.op(eng, lambda e: e.tensor_tensor(out=out, in0=in0, in1=in1, op=op), reads, writes)

        def stt(out, in0, scalar, in1, op0, op1, reads, writes, eng="dve"):
            return tr.op(eng, lambda e: e.scalar_tensor_tensor(out=out, in0=in0, scalar=scalar, in1=in1, op0=op0, op1=op1),
                         reads, writes)

        def mm(out, lhsT, rhs, start, stop, reads, writes):
            return tr.op("pe", lambda e: e.matmul(out, lhsT, rhs, start=start, stop=stop), reads, writes)

        tr.op("dve", lambda e: e.memset(ones_f, 1.0), (), ["ones_f"])
        tr.op("dve", lambda e: e.memset(ones_b, 1.0), (), ["ones_b"])
        tr.op("dve", lambda e: e.memset(epsT, EPS), (), ["eps"])
        dma("sp", vecs, vecs_d[:, :], (), ["vecs"])
        dma("pool", mask2, mask2_d[:, :], (), ["mask2"])
        dma("sp", triu, triu_d[:, :], (), ["triu"])
        dma("sp", invc, invc_d[:, :], (), ["invc"])
        for l in range(L):
            lam = vecs[:, l * NVL + V_LAM: l * NVL + V_LAM + 8]
            c1 = c1t[:, l * 8:(l + 1) * 8]
            c2 = c2t[:, l * 8:(l + 1) * 8]
            act(c1, lam, AF.Exp, ["vecs"], ["c1"], scale=-1.0)
            act(c1, c1, AF.Ln, ["c1"], ["c1"], scale=1.0, bias=1.0)
            tr.op("dve", lambda e, c1=c1, c2=c2: e.tensor_scalar_mul(out=c2, in0=c1, scalar1=-16.0), ["c1"], ["c2"])
            tr.op("dve", lambda e, c1=c1: e.tensor_scalar_mul(out=c1, in0=c1, scalar1=-8.0), ["c1", "c2"], ["c1"])

        o13 = [base]
        xt = carve([KC, T], F32, o13)
        hT = carve([KC, T], BF16, o13)
        big = carve([max(FC, 32), T], BF16, o13)
        sqr = Rot("sq", [carve([T], F32, o13) for _ in range(2)])
        sar = Rot("sa", [carve([T], F32, o13) for _ in range(2)])
        rstd = carve([T], F32, o13)
        base13 = o13[0]
        o1 = [base13]
        xcvr = Rot("xcv", [carve([3 + T], F32, o1) for _ in range(1)])
        halo_a = carve([8, 3], F32, o1)
        state = carve([8], F32, o1)
        xc = carve([T], F32, o1)
        xcb = big[:, 10, :]
        r_t = carve([T], F32, o1)
        ig_t = carve([T], F32, o1)
        a_t = carve([T], F32, o1)
        m_t = carve([T], F32, o1)
        hs_t = r_t
        gg_t = m_t
        yar = Rot("ya", [big[:, 11, :], big[:, 12, :]], [("big", 11), ("big", 12)])
        wabd = carve([8, 128], BF16, o1)
        wxbd = carve([8, 128], BF16, o1)
        xbv = carve([15 + T], F32, o1)
        halo_b = carve([8, 15], F32, o1)
        pP = carve([15 + T], F32, o1)
        pQ = carve([15 + T], F32, o1)
        pooled = big[:, 8:10, :]
        tmp15 = carve([16], F32, o1)
        poolw = carve([4, 2, 256], BF16, o1)
        ybr = Rot("yb", [big[:, 13, :], big[:, 14, :]], [("big", 13), ("big", 14)])
        qbr = Rot("qb", [big[:, 15, :], big[:, 16, :]], [("big", 15), ("big", 16)])
        rs_t = rstd
        vbr = Rot("vb", [carve([256], BF16, o1) for _ in range(2)])
        uT = big[:, 0:8, :]
        gvf = carve([1024], F32, o1)
        ss1 = carve([1], F32, o1)
        vvn = carve([1024], BF16, o1)
        junk = vvn
        sgn_t = carve([1024], F32, o1)
        sgb_t = carve([8, 128], F32, o1)
        wsT = carve([8, 128], BF16, o1)
        wsfr = Rot("wsf", [carve([128], F32, o1) for _ in range(2)])
        tmpd = Rot("tmpd", [carve([128], F32, o1) for _ in range(2)])
        ydbr = Rot("ydb", [carve([8, 128], BF16, o1) for _ in range(2)])
        o3 = [base13]
        mergedT = carve([KC, T], BF16, o3)
        sgr = Rot("sg", [carve([T], F32, o3) for _ in range(2)])
        tmr = Rot("tm", [carve([T], F32, o3) for _ in range(2)])
        acc = carve([T], F32, o3)
        ptile = carve([2, T], BF16, o3)
        o2 = [base]
        kTh = carve([S], BF16, o2)
        qTh = [carve([S], BF16, o2) for _ in range(3)]
        Vt = [carve([NBLK, 128], BF16, o2) for _ in range(3)]
        num_acc = carve([2048], F32, o2)
        den_acc = carve([2048], F32, o2)
        pbr = Rot("pb", [carve([256], BF16, o2) for _ in range(3)])
        ycb = carve([2048], BF16, o2)
        stg = Rot("stg", [carve([4096], BF16, o2) for _ in range(2)])

        def cast_weights(l):
            jobs = []
            for name, (C_, E_) in WSH.items():
                if E_ <= 4096:
                    g = 4096 // E_
                    for c0 in range(0, C_, g):
                        n = min(g, C_ - c0)
                        jobs.append((Wf[name][l, c0:c0 + n].rearrange("c p e -> p c e"),
                                     W[name][l % 2][c0:c0 + n].rearrange("c p e -> p c e"), n, E_))
                else:
                    for c in range(C_):
                        for e0 in range(0, E_, 4096):
                            e1 = min(E_, e0 + 4096)
                            jobs.append((Wf[name][l, c, :, e0:e1], W[name][l % 2][c, :, e0:e1], 0, e1 - e0))

            def view(sl, n, e):
                return sl[:, 0:n * e].rearrange("p (c e) -> p c e", e=e) if n else sl[:, 0:e]

            slots = []

            def load(j):
                src, dst, n, e = jobs[j]
                sl, sk = stg.next()
                slots.append((sl, sk))
                dma("pool", view(sl, n, e), src, (), [sk])

            def store(j):
                src, dst, n, e = jobs[j]
                sl, sk = slots[j]
                dma("pool", dst, view(sl, n, e), [sk], ())

            load(0)
            for j in range(len(jobs)):
                if j + 1 < len(jobs):
                    load(j + 1)
                store(j)

        def marker(e):
            return e.memset(epsT, EPS)

        def norm_to_hT(l, vo):
            pss, pk = psr.next()
            for kc in range(KC):
                sq, sk = sqr.next()
                act(sq, xt[:, kc, :], AF.Square, [("xt", kc)], [sk])
                mm(pss, ones_f, sq, kc == 0, kc == KC - 1, [sk, "ones_f"], [pk])
            act(rstd, pss, AF.Sqrt, [pk, "eps"], ["rstd"], scale=1.0 / D, bias=epsT)
            tr.op("dve", lambda e: e.reciprocal(out=rstd, in_=rstd), ["rstd"], ["rstd"])
            for kc in range(KC):
                stt(hT[:, kc, :], xt[:, kc, :], vcol(l, vo + kc), rstd, ALU.mult, ALU.mult,
                    [("xt", kc), "rstd", "vecs"], [("hT", kc)])

        qtog = [0]

        def altq():
            qtog[0] ^= 1
            return "sp" if qtog[0] else "pool"

        def lin(wap, nk, rhs_fn, rhs_keys, q="pool"):
            wt, wk = wA.next()
            dma(q, wt[:, 0:nk, :], wap, (), [wk])
            pb, pk = psr.next()
            for kc in range(nk):
                mm(pb, wt[:, kc, :], rhs_fn(kc), kc == 0, kc == nk - 1, [wk, rhs_keys(kc)], [pk])
            return pb, pk

        def proj(l, col):
            return lin(wv3("w_in", l, col // 128), KC, lambda kc: hT[:, kc, :], lambda kc: ("hT", kc))

        def ffn(l, pre, vo):
            norm_to_hT(l, vo)
            for fc in range(FC):
                pa, pak = lin(wv3(pre + "_w1", l, fc), KC, lambda kc: hT[:, kc, :], lambda kc: ("hT", kc))
                pb, pbk = lin(wv3(pre + "_w3", l, fc), KC, lambda kc: hT[:, kc, :], lambda kc: ("hT", kc), q="sp")
                sa, sk = sar.next()
                act(sa, pa, AF.Silu, [pak], [sk])
                tt("dve", big[:, fc, :], sa, pb, ALU.mult, [sk, pbk], [("big", fc)])
            for dc in range(KC):
                wt, wk = wB.next()
                wv = wt[:, 0:FC * 128].rearrange("p (f n) -> p f n", n=128)
                dma("sp" if dc % 2 else "pool", wv, wv3(pre + "_w2", l, dc), (), [wk])
                po, pk = psr.next()
                for fc in range(FC):
                    mm(po, wv[:, fc, :], big[:, fc, :], fc == 0, fc == FC - 1, [wk, ("big", fc)], [pk])
                stt(xt[:, dc, :], po, 0.5, xt[:, dc, :], ALU.mult, ALU.add, [pk, ("xt", dc)], [("xt", dc)])

        def load_xt(src, c0):
            dma("sp", xt, src[:, c0:c0 + T].rearrange("(kc p) t -> p kc t", p=128), (), [("xt", kc) for kc in range(KC)])

        def store_xt(dst, c0):
            dma("sp", dst[:, c0:c0 + T].rearrange("(kc p) t -> p kc t", p=128), xt, [("xt", kc) for kc in range(KC)], ())

        def qk_norm(pq, pqk, gcol, dst_ap):
            sq, sk = sqr.next()
            act(sq, pq, AF.Square, [pqk], [sk])
            p2, p2k = psr.next()
            mm(p2, ones_f, sq, True, True, [sk, "ones_f"], [p2k])
            act(rs_t, p2, AF.Sqrt, [p2k, "eps"], ["rstd"], scale=1.0 / 128, bias=epsT)
            tr.op("dve", lambda e: e.reciprocal(out=rs_t, in_=rs_t), ["rstd"], ["rstd"])
            qb, qk = qbr.next()
            stt(qb, pq, gcol, rs_t, ALU.mult, ALU.mult, [pqk, "rstd", "vecs"], [qk])
            dma("sp", dst_ap, qb, [qk], ())

        def layer_setup(l):
            tr.op("dve", lambda e: e.memset(wabd, 0.0), (), ["wabd"])
            tr.op("dve", lambda e: e.memset(wxbd, 0.0), (), ["wxbd"])
            for hh in range(16):
                ci, half = hh // 2, hh % 2
                dma("pool", wabd[half * 64:(half + 1) * 64, ci, half * 64:(half + 1) * 64], W["rg_wa"][l, hh], (), ["wabd"])
                dma("pool", wxbd[half * 64:(half + 1) * 64, ci, half * 64:(half + 1) * 64], W["rg_wx"][l, hh], (), ["wxbd"])
            for g in range(4):
                dma("pool", poolw[:, g, :, :], W["pool_w"][l, g].rearrange("(c p) d -> p c d", p=128), (), ["poolw"])
            dma("sp", sgn_t, sgn_d[l], (), ["sgn"])
            dma("sp", sgb_t, sgb_d[l].rearrange("p (g t) -> p g t", t=128), (), ["sgb"])
            for g in range(8):
                wf, wfk = wsfr.next()
                dma("sp", wf, sgwT_d[l, g], (), [wfk])
                tt("dve", wsT[:, g, :], wf, triu, ALU.mult, [wfk, "triu"], ["wsT"])

        def mixer_A(l, t, c0):
            for ci in range(8):
                px, pxk = proj(l, O_XA + ci * 128)
                pg, pgk = proj(l, O_GA + ci * 128)
                xcv, xk = xcvr.next()
                if t == 0:
                    tr.op("dve", lambda e, xcv=xcv: e.memset(xcv[:, 0:3], 0.0), (), [xk])
                else:
                    tr.op("dve", lambda e, xcv=xcv, ci=ci: e.tensor_copy(out=xcv[:, 0:3], in_=halo_a[:, ci, :]),
                          [("haloa", ci)], [xk])
                act(xcv[:, 3:3 + T], px, AF.Copy, [pxk], [xk])
                tr.op("dve", lambda e, xcv=xcv, ci=ci: e.tensor_copy(out=halo_a[:, ci, :], in_=xcv[:, T:T + 3]),
                      [xk], [("haloa", ci)])
                w0, b0 = vcol(l, V_CW + ci), vcol(l, V_CB + ci)
                tr.op("dve", lambda e, xcv=xcv, w0=w0, b0=b0: e.tensor_scalar(
                    out=xc, in0=xcv[:, 3:3 + T], scalar1=w0, scalar2=b0, op0=ALU.mult, op1=ALU.add), [xk, "vecs"], ["xc"])
                for j in range(1, 4):
                    stt(xc, xcv[:, 3 - j:3 - j + T], vcol(l, V_CW + j * 8 + ci), xc, ALU.mult, ALU.add,
                        [xk, "xc", "vecs"], ["xc"])
                act(xcb, xc, AF.Copy, ["xc"], [("big", 10)])
                pr, prk = psr.next()
                mm(pr, wabd[:, ci, :], xcb, True, True, ["wabd", ("big", 10)], [prk])
                pi, pik = psr.next()
                mm(pi, wxbd[:, ci, :], xcb, True, True, ["wxbd", ("big", 10)], [pik])
                act(r_t, pr, AF.Sigmoid, [prk, "vecs"], ["r"], bias=vcol(l, V_BA + ci))
                act(ig_t, pi, AF.Sigmoid, [pik, "vecs"], ["ig"], bias=vcol(l, V_BX + ci))
                act(a_t, r_t, AF.Exp, ["r", "c1"], ["a"], scale=c1t[:, l * 8 + ci:l * 8 + ci + 1])
                act(m_t, r_t, AF.Exp, ["r", "c2"], ["m"], scale=c2t[:, l * 8 + ci:l * 8 + ci + 1])
                act(m_t, m_t, AF.Sqrt, ["m"], ["m"], scale=-1.0, bias=1.0)
                tt("dve", ig_t, ig_t, m_t, ALU.mult, ["ig", "m"], ["ig"])
                tt("dve", ig_t, ig_t, xc, ALU.mult, ["ig", "xc"], ["ig"])
                if t == 0:
                    tr.op("dve", lambda e, ci=ci: e.memset(state[:, ci:ci + 1], 0.0), (), [("st", ci)])
                tr.op("dve", lambda e, ci=ci: e.tensor_tensor_scan(
                    out=hs_t, data0=a_t, data1=ig_t, initial=state[:, ci:ci + 1], op0=ALU.mult, op1=ALU.add),
                    ["a", "ig", ("st", ci)], ["r"])
                tr.op("dve", lambda e, ci=ci: e.tensor_copy(out=state[:, ci:ci + 1], in_=hs_t[:, T - 1:T]),
                      ["r"], [("st", ci)])
                act(gg_t, pg, AF.Gelu, [pgk], ["m"])
                ya, yk = yar.next()
                tt("dve", ya, hs_t, gg_t, ALU.mult, ["r", "m"], [yk])
                dma("sp", yT[0][ci * 128:(ci + 1) * 128, c0:c0 + T], ya, [yk], ())

        def mixer_B(l, t, c0):
            for g in range(4):
                win = 2 << g
                for j in range(2):
                    ch = 2 * g + j
                    pb_, pbk = proj(l, O_XB + ch * 128)
                    if t == 0:
                        tr.op("dve", lambda e: e.memset(xbv[:, 0:15], 0.0), (), ["xbv"])
                    else:
                        tr.op("dve", lambda e, ch=ch: e.tensor_copy(out=xbv[:, 0:15], in_=halo_b[:, ch, :]),
                              [("halob", ch)], ["xbv"])
                    act(xbv[:, 15:15 + T], pb_, AF.Copy, [pbk], ["xbv"])
                    tr.op("dve", lambda e, ch=ch: e.tensor_copy(out=halo_b[:, ch, :], in_=xbv[:, T:T + 15]),
                          ["xbv"], [("halob", ch)])
                    srcb, sk_ = xbv, "xbv"
                    bufs = [(pP, "pP"), (pQ, "pQ")]
                    for k in range(1, g + 2):
                        lo, sh = (1 << k) - 1, 1 << (k - 1)
                        dstb, dk_ = bufs[k % 2]
                        tt("dve", dstb[:, lo:], srcb[:, lo:], srcb[:, lo - sh:15 + T - sh], ALU.add, [sk_], [dk_])
                        srcb, sk_ = dstb, dk_
                    stt(pooled[:, j, :], srcb[:, 15:], 1.0 / win, xbv[:, 15:], ALU.mult, ALU.subtract,
                        [sk_, "xbv"], [("big", 8 + j)])
                    if t == 0:
                        w1_ = win - 1
                        tt("dve", tmp15[:, 0:w1_], srcb[:, 15:15 + w1_], invc[:, g * 16:g * 16 + w1_], ALU.mult,
                           [sk_, "invc"], ["tmp15"])
                        tt("dve", pooled[:, j, 0:w1_], tmp15[:, 0:w1_], xbv[:, 15:15 + w1_], ALU.subtract,
                           ["tmp15", "xbv"], [("big", 8 + j)])
                for dch in range(2):
                    pp, ppk = psr.next()
                    for cch in range(2):
                        mm(pp, poolw[:, g, cch, dch * 128:(dch + 1) * 128], pooled[:, cch, :], cch == 0, cch == 1,
                           ["poolw", ("big", 8 + cch)], [ppk])
                    yb, ybk = ybr.next()
                    act(yb, pp, AF.Identity, [ppk, "vecs"], [ybk], scale=vcol(l, V_PS + 2 * g + dch))
                    ch = 2 * g + dch
                    dma("sp", yT[1][ch * 128:(ch + 1) * 128, c0:c0 + T], yb, [ybk], ())

        def tokmajor_slab(l, col):
            wt, wk = wB.next()
            wv = wt[:, 0:4096].rearrange("p (k n) -> p k n", n=256)
            dma("pool", wv[:, :, 0:128], wv3("w_in", l, col // 128), (), [wk])
            dma("pool", wv[:, :, 128:256], wv3("w_in", l, col // 128 + 1), (), [wk])
            return wv, wk

        def mixer_C_inputs(l, t, c0):
            for hq in range(24):
                pq, pqk = proj(l, O_Q + hq * 128)
                qk_norm(pq, pqk, vcol(l, V_QG), qTd[hq * 128:(hq + 1) * 128, c0:c0 + T])
            for hk in range(8):
                pq, pqk = proj(l, O_K + hk * 128)
                qk_norm(pq, pqk, vcol(l, V_KG), kTd[hk * 128:(hk + 1) * 128, c0:c0 + T])
            for qv in range(4):
                wv, wk = tokmajor_slab(l, O_V + qv * 256)
                for tg in range(NTG):
                    pv, pvk = psr.next()
                    for kc in range(KC):
                        mm(pv[:, 0:256], hT[:, kc, tg * 128:(tg + 1) * 128], wv[:, kc, :], kc == 0, kc == KC - 1,
                           [wk, ("hT", kc)], [pvk])
                    vb, vk = vbr.next()
                    act(vb, pv[:, 0:256], AF.Copy, [pvk], [vk])
                    dma("sp", vtm[c0 + tg * 128:c0 + (tg + 1) * 128, qv * 256:(qv + 1) * 256], vb, [vk], ())

        def mixer_D(l, t, c0):
            for g in range(8):
                pu, puk = proj(l, O_U + g * 128)
                act(uT[:, g, :], pu, AF.Gelu, [puk], [("big", g)])
            for tg in range(NTG):
                pva, pvak = psr.next()
                pvb, pvbk = psr.next()
                for qv in range(4):
                    wv, wk = tokmajor_slab(l, O_VV + qv * 256)
                    pdst, pdk = (pva, pvak) if qv < 2 else (pvb, pvbk)
                    for kc in range(KC):
                        mm(pdst[:, (qv % 2) * 256:(qv % 2 + 1) * 256], hT[:, kc, tg * 128:(tg + 1) * 128], wv[:, kc, :],
                           kc == 0, kc == KC - 1, [wk, ("hT", kc)], [pdk])
                act(gvf[:, 0:512], pva, AF.Gelu, [pvak], ["gvf"])
                act(gvf[:, 512:1024], pvb, AF.Gelu, [pvbk], ["gvf"])
                act(junk, gvf, AF.Square, ["gvf"], ["vvn", "ss1"], accum_out=ss1)
                act(ss1, ss1, AF.Sqrt, ["ss1", "eps"], ["ss1"], scale=1.0 / 1024, bias=epsT)
                tr.op("dve", lambda e: e.reciprocal(out=ss1, in_=ss1), ["ss1"], ["ss1"])
                stt(vvn, gvf, ss1, sgn_t, ALU.mult, ALU.mult, ["gvf", "ss1", "sgn"], ["vvn"])
                ydb, ydk = ydbr.next()
                for g in range(8):
                    pm, pmk = psr.next()
                    mm(pm[:, 0:128], vvn[:, g * 128:(g + 1) * 128], wsT[:, g, :], True, True, ["vvn", "wsT"], [pmk])
                    td, tk = tmpd.next()
                    tt("dve", td, pm[:, 0:128], sgb_t[:, g, :], ALU.add, [pmk, "sgb"], [tk])
                    tt("dve", ydb[:, g, :], td, uT[:, g, tg * 128:(tg + 1) * 128], ALU.mult, [tk, ("big", g)], [ydk])
                dma("sp", yT[3][:, c0 + tg * 128:c0 + (tg + 1) * 128].rearrange("(g p) t -> p g t", p=128), ydb, [ydk], ())

        def attention(l):
            scale = 128.0 ** -0.5
            for h in range(8):
                dma("sp", kTh, kTd[h * 128:(h + 1) * 128, :], (), ["kTh"])
                for g in range(3):
                    dma("sp", qTh[g], qTd[(g * 8 + h) * 128:(g * 8 + h + 1) * 128, :], (), [("qTh", g)])
                for g, dil in enumerate((1, 4, 16)):
                    span = 128 * dil
                    for j in range(S // span):
                        dma("sp", Vt[g][:, j * dil:(j + 1) * dil, :],
                            vtm[j * span:(j + 1) * span, h * 128:(h + 1) * 128].rearrange("(i r) d -> i r d", r=dil),
                            (), [("Vt", g, j)])
                for w in range(S // 2048):
                    for g, dil in enumerate((1, 4, 16)):
                        span = 128 * dil
                        ext = 127 * dil + 1
                        for j in range(w * 2048 // span, (w + 1) * 2048 // span):
                            for r in range(dil):
                                b0 = j * span + r
                                qsl = qTh[g][:, b0:b0 + ext:dil]
                                pss_, psk = psr.next()
                                lo = 0 if j > 0 else 128
                                if j > 0:
                                    mm(pss_[:, 0:128], kTh[:, b0 - span:b0 - span + ext:dil], qsl, True, True,
                                       ["kTh", ("qTh", g)], [psk])
                                mm(pss_[:, 128:256], kTh[:, b0:b0 + ext:dil], qsl, True, True, ["kTh", ("qTh", g)], [psk])
                                pb_, pbk = pbr.next()
                                act(pb_[:, lo:256], pss_[:, lo:256], AF.Exp, [psk], [pbk], scale=scale)
                                tt("dve", pb_[:, lo:256], pb_[:, lo:256], mask2[:, lo:256], ALU.mult, [pbk, "mask2"], [pbk])
                                pn, pnk = psr.next()
                                pd, pdk = psr.next()
                                blk = j * dil + r
                                if j > 0:
                                    mm(pn[:, 0:128], Vt[g][:, blk - dil, :], pb_[:, 0:128], True, False,
                                       [("Vt", g, j - 1), pbk], [pnk])
                                    mm(pd[:, 0:128], ones_b, pb_[:, 0:128], True, False, ["ones_b", pbk], [pdk])
                                mm(pn[:, 0:128], Vt[g][:, blk, :], pb_[:, 128:256], j == 0, True, [("Vt", g, j), pbk], [pnk])
                                mm(pd[:, 0:128], ones_b, pb_[:, 128:256], j == 0, True, ["ones_b", pbk], [pdk])
                                wb0 = b0 - w * 2048
                                nsl = num_acc[:, wb0:wb0 + ext:dil]
                                dsl = den_acc[:, wb0:wb0 + ext:dil]
                                if g == 0:
                                    act(nsl, pn[:, 0:128], AF.Copy, [pnk], ["num"])
                                    act(dsl, pd[:, 0:128], AF.Copy, [pdk], ["den"])
                                else:
                                    tt("dve", nsl, pn[:, 0:128], nsl, ALU.add, [pnk, "num"], ["num"])
                                    tt("dve", dsl, pd[:, 0:128], dsl, ALU.add, [pdk, "den"], ["den"])
                    tr.op("dve", lambda e: e.reciprocal(out=den_acc, in_=den_acc), ["den"], ["den"])
                    tt("dve", ycb, num_acc, den_acc, ALU.mult, ["num", "den"], ["ycb"])
                    dma("sp", yT[2][h * 128:(h + 1) * 128, w * 2048:(w + 1) * 2048], ycb, ["ycb"], ())

        def pass3_tile(l, t, c0, dst):
            load_xt(xs, c0)
            norm_to_hT(l, V_MIX)
            for b in range(4):
                dma("sp", big[:, 8 * b:8 * b + 8, :], yT[b][:, c0:c0 + T].rearrange("(kc p) t -> p kc t", p=128),
                    (), [("big", 8 * b + kc) for kc in range(8)])
            for dc in range(KC):
                for b in range(4):
                    pg, pgk = lin(wv3("w_in", l, (O_G + b * 2048 + dc * 128) // 128), KC,
                                  lambda kc: hT[:, kc, :], lambda kc: ("hT", kc), q="sp")
                    sg, sgk = sgr.next()
                    act(sg, pg, AF.Sigmoid, [pgk, "vecs"], [sgk], bias=vcol(l, V_BG + b * 16 + dc))
                    py, pyk = lin(wv3("w_branch", l, b * KC + dc), 8,
                                  lambda kc, b=b: big[:, 8 * b + kc, :], lambda kc, b=b: ("big", 8 * b + kc))
                    if b == 0:
                        tt("dve", acc, py, sg, ALU.mult, [pyk, sgk], ["acc"])
                    else:
                        tm, tmk = tmr.next()
                        tt("dve", tm, py, sg, ALU.mult, [pyk, sgk], [tmk])
                        if b < 3:
                            tt("dve", acc, acc, tm, ALU.add, ["acc", tmk], ["acc"])
                        else:
                            tt("dve", mergedT[:, dc, :], acc, tm, ALU.add, ["acc", tmk], [("mg", dc)])
            for dc in range(KC):
                po, pok = lin(wv3("w_out", l, dc), KC,
                              lambda kc: mergedT[:, kc, :], lambda kc: ("mg", kc), q=altq())
                tt("dve", xt[:, dc, :], po, xt[:, dc, :], ALU.add, [pok, ("xt", dc)], [("xt", dc)])
            ffn(l, "ffn2", V_F2)
            norm_to_hT(l, V_PLE)
            dma("pool", ptile, pT[l, :, c0:c0 + T].rearrange("(kc p) t -> p kc t", p=128), (), ["ptile"])
            for dc in range(KC):
                pg, pgk = lin(wv3("ple_gate_w", l, dc), KC,
                              lambda kc: hT[:, kc, :], lambda kc: ("hT", kc), q="sp")
                sg, sgk = sgr.next()
                act(sg, pg, AF.Sigmoid, [pgk], [sgk])
                pp, ppk = lin(wv3("ple_proj", l, dc), 2,
                              lambda kc: ptile[:, kc, :], lambda kc: "ptile")
                tm, tmk = tmr.next()
                tt("dve", tm, pp, sg, ALU.mult, [ppk, sgk], [tmk])
                tt("dve", xt[:, dc, :], xt[:, dc, :], tm, ALU.add, [("xt", dc), tmk], [("xt", dc)])
            store_xt(dst, c0)

        cast_weights(0)
        tr.barrier(marker)
        for l in range(L):
            src = xT_in if l == 0 else xs
            layer_setup(l)
            for t in range(NT):
                c0 = t * T
                load_xt(src, c0)
                ffn(l, "ffn1", V_F1)
                norm_to_hT(l, V_MIX)
                mixer_A(l, t, c0)
                mixer_B(l, t, c0)
                mixer_C_inputs(l, t, c0)
                mixer_D(l, t, c0)
                store_xt(xs, c0)
            tr.barrier(marker)
            if l + 1 < L:
                cast_weights(l + 1)
            attention(l)
            tr.barrier(marker)
            dst = outT if l == L - 1 else xs
            for t in range(NT):
                pass3_tile(l, t, t * T, dst)
            tr.barrier(marker)

        with nc.Block() as block:
            @block.tensor
            def _(e):
                tr.emit("pe", e)

            @block.scalar
            def _(e):
                tr.emit("act", e)

            @block.vector
            def _(e):
                tr.emit("dve", e)

            @block.gpsimd
            def _(e):
                tr.emit("pool", e)

            @block.sync
            def _(e):
                tr.emit("sp", e)
    return nc


def host_consts():
    ki = np.arange(128)[:, None]
    qi = np.arange(128)[None, :]
    mask2 = np.concatenate([(ki >= qi), (ki <= qi)], axis=1).astype(np.float32)
    triu = (ki <= qi).astype(np.float32)
    invc = np.ones((128, 64), np.float32)
    for g, win in enumerate((2, 4, 8, 16)):
        for pos in range(16):
            invc[:, g * 16 + pos] = 1.0 / min(pos + 1, win)
    return mask2, triu, invc


def _cols(v):
    v = np.asarray(v, np.float32).reshape(-1, 128)
    return v.T


def host_layout(inp, L):
    vecs = np.zeros((128, L * NVL), np.float32)
    for l in range(L):
        o = l * NVL
        vecs[:, o + V_F1:o + V_F1 + 16] = _cols(inp["ffn1_norm"][l])
        vecs[:, o + V_MIX:o + V_MIX + 16] = _cols(inp["mix_norm"][l])
        vecs[:, o + V_F2:o + V_F2 + 16] = _cols(inp["ffn2_norm"][l])
        vecs[:, o + V_PLE:o + V_PLE + 16] = _cols(inp["ple_norm"][l])
        for b in range(4):
            vecs[:, o + V_BG + b * 16:o + V_BG + (b + 1) * 16] = _cols(inp["b_gate"][l, b])
        for j in range(4):
            vecs[:, o + V_CW + j * 8:o + V_CW + (j + 1) * 8] = _cols(inp["conv_w"][l, j])
        vecs[:, o + V_CB:o + V_CB + 8] = _cols(inp["conv_b"][l])
        vecs[:, o + V_BA:o + V_BA + 8] = _cols(inp["rg_ba"][l])
        vecs[:, o + V_BX:o + V_BX + 8] = _cols(inp["rg_bx"][l])
        vecs[:, o + V_LAM:o + V_LAM + 8] = _cols(inp["rg_lambda"][l])
        vecs[:, o + V_PS:o + V_PS + 8] = _cols(inp["pool_scale"][l])
        vecs[:, o + V_QG:o + V_QG + 1] = _cols(inp["q_gain"][l])
        vecs[:, o + V_KG:o + V_KG + 1] = _cols(inp["k_gain"][l])
    sgn = np.ascontiguousarray(np.broadcast_to(np.asarray(inp["sg_norm"], np.float32)[:, None, :], (L, 128, 1024)))
    sgb = np.ascontiguousarray(np.broadcast_to(
        np.asarray(inp["sg_b"], np.float32).reshape(L, 1, 1024), (L, 128, 1024)))
    sgwT = np.ascontiguousarray(np.asarray(inp["sg_w"], np.float32).transpose(0, 1, 3, 2))
    return vecs, sgn, sgb, sgwT


def _tile_w(w):
    w = np.asarray(w, np.float32)
    lead = w.shape[:-2]
    K_, N_ = w.shape[-2:]
    w = w.reshape(*lead, K_ // 128, 128, N_ // 128, 128)
    nl = len(lead)
    w = w.transpose(*range(nl), nl + 2, nl + 1, nl, nl + 3)
    return np.ascontiguousarray(w)


def _flat_w(t):
    return t.reshape(t.shape[0], -1, 128, t.shape[-2] * t.shape[-1])


_TILED = ("ffn1_w1", "ffn1_w3", "ffn1_w2", "w_in", "w_branch", "w_out", "ffn2_w1", "ffn2_w3", "ffn2_w2",
          "ple_gate_w", "ple_proj")
_WNAMES = ("ffn1_w1", "ffn1_w3", "ffn1_w2", "w_in", "rg_wa", "rg_wx", "pool_w", "w_branch", "w_out",
           "ffn2_w1", "ffn2_w3", "ffn2_w2", "ple_gate_w", "ple_proj")


def run(inp, trace=False, per_layer=False):
    x = np.asarray(inp["x"], np.float32)
    B, S, _ = x.shape
    L = inp["w_in"].shape[0]
    DFF = inp["ffn1_w1"].shape[2]
    mask2, triu, invc = host_consts()
    p = np.asarray(inp["p"], np.float32)
    xT = [np.ascontiguousarray(x[b].T) for b in range(B)]
    groups = [[l] for l in range(L)] if per_layer else [list(range(L))]
    nc = build_program(S, len(groups[0]), DFF)
    res = None
    for grp in groups:
        sub = {k: np.asarray(v)[grp[0]:grp[-1] + 1] for k, v in inp.items() if k != "x"}
        vecs, sgn, sgb, sgwT = host_layout(sub, len(grp))
        shared = {n: (_flat_w(_tile_w(sub[n])) if n in _TILED else np.ascontiguousarray(np.asarray(sub[n], np.float32)))
                  for n in _WNAMES}
        shared.update(vecs=vecs, sgn=sgn, sgb=sgb, sgwT=sgwT, mask2=mask2, triu=triu, invc=invc)
        in_maps = []
        for b in range(B):
            m = dict(shared)
            m["xT"] = xT[b]
            m["pT"] = np.ascontiguousarray(p[grp[0]:grp[-1] + 1, b].transpose(0, 2, 1))
            in_maps.append(m)
        res = run_bass_kernel_spmd(nc, in_maps, core_ids=list(range(B)), trace=trace)
        xT = [np.ascontiguousarray(res.results[b]["outT"]) for b in range(B)]
    out = np.stack([np.ascontiguousarray(xT[b].T) for b in range(B)], axis=0)
    return out.astype(np.float32), res


def kernel(**inputs):
    out, _ = run(inputs)
    return out
```
